# Optimizing a Trainium2 kernel written in Bass

```python
import jax, jax.numpy as jnp
from jax import lax
import numpy as np

D_MODEL = 1024
BATCH = 16
SEQ = 4096
DEPTH = 1

GRID_W = 64
NA_WIDTH = D_MODEL // 2
NA_HEADS = 8
NA_HEAD_DIM = NA_WIDTH // NA_HEADS
WIN_ROWS_MAX = 8
WIN_COLS = 16
POOL_WIDTH = D_MODEL - NA_WIDTH
POOL_WINDOWS = (2, 4, 8, 16)
POOL_GROUPS = len(POOL_WINDOWS)
POOL_GROUP_DIM = POOL_WIDTH // POOL_GROUPS
MIX_WIDTH = NA_WIDTH + POOL_WIDTH
IN_WIDTH = 3 * NA_WIDTH + POOL_WIDTH
D_FF = -(-8 * D_MODEL // (3 * 256)) * 256
EPS = 1e-6

kernel_name = "hybrid_neighbourhood_attn_multiscale_pool_block"


def rms_norm(x, g):
    xf = x.astype(jnp.float32)
    y = xf * lax.rsqrt(jnp.mean(xf * xf, axis=-1, keepdims=True) + EPS)
    return (y * g.astype(jnp.float32)).astype(x.dtype)


def neighbourhood_attention(q, k, v, rpb):
    B, T, H, Dh = q.shape
    rows = T // GRID_W
    wr = min(WIN_ROWS_MAX, rows)
    q = q.reshape(B, rows, GRID_W, H, Dh)
    k = k.reshape(B, rows, GRID_W, H, Dh)
    v = v.reshape(B, rows, GRID_W, H, Dh)
    row_start = jnp.clip(jnp.arange(rows) - wr // 2, 0, rows - wr)
    cols = jnp.arange(GRID_W)
    col_idx = jnp.clip(cols - WIN_COLS // 2, 0, GRID_W - WIN_COLS)[:, None] + jnp.arange(WIN_COLS)
    col_rel = col_idx - cols[:, None] + (WIN_COLS - 1)
    rpb_cols = rpb.astype(jnp.float32)[:, :, col_rel]
    scale = Dh ** -0.5

    def one_row(r):
        rs = row_start[r]
        kb = lax.dynamic_slice_in_dim(k, rs, wr, axis=1)
        vb = lax.dynamic_slice_in_dim(v, rs, wr, axis=1)
        kg = kb[:, :, col_idx]
        vg = vb[:, :, col_idx]
        qr = lax.dynamic_index_in_dim(q, r, axis=1, keepdims=False)
        row_rel = rs + jnp.arange(wr) - r + (WIN_ROWS_MAX - 1)
        bias = jnp.transpose(rpb_cols[:, row_rel], (0, 2, 1, 3))
        s = jnp.einsum('bqhd,bwqkhd->bhqwk', qr, kg,
                       preferred_element_type=jnp.float32) * scale + bias[None]
        p = jax.nn.softmax(s.reshape(B, H, GRID_W, wr * WIN_COLS), axis=-1)
        p = p.reshape(B, H, GRID_W, wr, WIN_COLS).astype(v.dtype)
        return jnp.einsum('bhqwk,bwqkhd->bqhd', p, vg)

    out = lax.map(one_row, jnp.arange(rows))
    return jnp.moveaxis(out, 0, 1).reshape(B, T, H * Dh)


def multiscale_pool(u, w_pool, pool_scale):
    B, T, C = u.shape
    uf = u.astype(jnp.float32)
    csum = jnp.concatenate([jnp.zeros((B, 1, C), jnp.float32), jnp.cumsum(uf, axis=1)], axis=1)
    t = jnp.arange(T)
    outs = []
    for g, w in enumerate(POOL_WINDOWS):
        sl = slice(g * POOL_GROUP_DIM, (g + 1) * POOL_GROUP_DIM)
        lo = jnp.clip(t - w // 2, 0, T)
        hi = jnp.clip(t + w // 2, 0, T)
        cg = csum[:, :, sl]
        mean = (cg[:, hi] - cg[:, lo]) / (hi - lo).astype(jnp.float32)[:, None]
        outs.append(mean - uf[:, :, sl])
    d = jnp.stack(outs, axis=2).astype(u.dtype)
    y = jnp.einsum('btgc,gcd->btgd', d, w_pool).reshape(B, T, C)
    return y * pool_scale


def setup_inputs(seed: int = 0) -> dict:
    key = jax.random.key(seed)
    ks = jax.random.split(key, 14)
    f32 = jnp.float32
    nrm = lambda k, shape, fan_in: jax.random.normal(k, shape, f32) * fan_in ** -0.5
    gain = lambda k, shape: 1.0 + 0.02 * jax.random.normal(k, shape, f32)
    return {
        "x": jax.random.normal(ks[0], (BATCH, SEQ, D_MODEL), f32),
        "norm1_g": gain(ks[1], (DEPTH, D_MODEL)),
        "w_in": nrm(ks[2], (DEPTH, D_MODEL, IN_WIDTH), D_MODEL),
        "q_norm_g": gain(ks[3], (DEPTH, NA_HEAD_DIM)),
        "k_norm_g": gain(ks[4], (DEPTH, NA_HEAD_DIM)),
        "rpb": 0.1 * jax.random.normal(ks[5], (DEPTH, NA_HEADS, 2 * WIN_ROWS_MAX - 1, 2 * WIN_COLS - 1), f32),
        "w_pool": nrm(ks[6], (DEPTH, POOL_GROUPS, POOL_GROUP_DIM, POOL_GROUP_DIM), POOL_GROUP_DIM),
        "pool_scale": gain(ks[7], (DEPTH, POOL_WIDTH)),
        "w_out": nrm(ks[8], (DEPTH, MIX_WIDTH, D_MODEL), MIX_WIDTH),
        "norm2_g": gain(ks[9], (DEPTH, D_MODEL)),
        "w_gate": nrm(ks[10], (DEPTH, D_MODEL, D_FF), D_MODEL),
        "w_up": nrm(ks[11], (DEPTH, D_MODEL, D_FF), D_MODEL),
        "w_down": nrm(ks[12], (DEPTH, D_FF, D_MODEL), D_FF),
    }


def reference(x, norm1_g, w_in, q_norm_g, k_norm_g, rpb, w_pool, pool_scale, w_out,
              norm2_g, w_gate, w_up, w_down):
    B, T, _ = x.shape
    for l in range(DEPTH):
        h = rms_norm(x, norm1_g[l])
        proj = h @ w_in[l]
        q = proj[..., :NA_WIDTH].reshape(B, T, NA_HEADS, NA_HEAD_DIM)
        k = proj[..., NA_WIDTH:2 * NA_WIDTH].reshape(B, T, NA_HEADS, NA_HEAD_DIM)
        v = proj[..., 2 * NA_WIDTH:3 * NA_WIDTH].reshape(B, T, NA_HEADS, NA_HEAD_DIM)
        u = proj[..., 3 * NA_WIDTH:]
        q = rms_norm(q, q_norm_g[l])
        k = rms_norm(k, k_norm_g[l])
        a = neighbourhood_attention(q, k, v, rpb[l])
        p = multiscale_pool(u, w_pool[l], pool_scale[l])
        x = x + jnp.concatenate([a, p], axis=-1) @ w_out[l]
        h2 = rms_norm(x, norm2_g[l])
        x = x + (jax.nn.silu(h2 @ w_gate[l]) * (h2 @ w_up[l])) @ w_down[l]
    return x
```

```python
import numpy as np
import concourse.bass as bass
import concourse.mybir as mybir
from concourse.bass_utils import run_bass_kernel_spmd

F32 = mybir.dt.float32
BF16 = mybir.dt.bfloat16
ALU = mybir.AluOpType
AF = mybir.ActivationFunctionType
AX = mybir.AxisListType

N_CORES = 8
TOK = 8192
SEQ = 4096
D = 1024
NBLK = 16
DFF = 2816
NFC = 22
EPS = 1e-6
NEG = -30000.0
SEM_ROT = 30000
import os
SKIP = set(os.environ.get('KSKIP', '').split(','))


class Prog:
    ENG = ("sp", "act", "pe", "dve", "pool")

    def __init__(self):
        self.ops = []
        self.tiles = {}
        self.dcount = {}
        self.last = {e: None for e in self.ENG}
        self.bar = {e: None for e in self.ENG}

    def op(self, eng, fn, r=(), w=(), dma=None):
        i = len(self.ops)
        deps = set()
        ddeps = {}

        def add(j):
            o = self.ops[j]
            if o["dma"] is not None:
                c = o["dma"]
                ddeps[c] = max(ddeps.get(c, 0), self.dcount[c])
            else:
                deps.add(j)

        for t in r:
            st = self.tiles.setdefault(t, [None, []])
            if st[0] is not None:
                add(st[0])
        for t in w:
            st = self.tiles.setdefault(t, [None, []])
            if st[0] is not None:
                add(st[0])
            for j in st[1]:
                add(j)
        for t in r:
            self.tiles[t][1].append(i)
        for t in w:
            st = self.tiles[t]
            st[0] = i
            st[1] = []
        if self.bar[eng] is not None:
            bd, bdd = self.bar[eng]
            deps |= bd
            for c, v in bdd.items():
                ddeps[c] = max(ddeps.get(c, 0), v)
            self.bar[eng] = None
        latest = {}
        for j in deps:
            en = self.ops[j]["eng"]
            if en not in latest or j > latest[en]:
                latest[en] = j
        deps = set(latest.values())
        if dma is not None:
            self.dcount[dma] = self.dcount.get(dma, 0) + 1
        self.ops.append(dict(eng=eng, fn=fn, deps=deps, ddeps=ddeps, dma=dma))
        self.last[eng] = i
        return i

    def barrier(self):
        bd = set(j for j in self.last.values() if j is not None and self.ops[j]["dma"] is None)
        for e in self.ENG:
            for j in range(len(self.ops) - 1, -1, -1):
                if self.ops[j]["eng"] == e and self.ops[j]["dma"] is None and self.ops[j]["fn"] is not None:
                    bd.add(j)
                    break
        bdd = dict(self.dcount)
        for e in self.ENG:
            self.bar[e] = (set(bd), dict(bdd))

    def final_wait(self, eng="sp"):
        self.bar[eng] = (set(), dict(self.dcount))
        self.op(eng, None)

    def emit(self, nc):
        ops = self.ops
        need = [False] * len(ops)
        for o in ops:
            for j in o["deps"]:
                if not (o["eng"] == "pe" and ops[j]["eng"] == "pe"):
                    need[j] = True
        cnt = {e: 0 for e in self.ENG}
        sig = [None] * len(ops)
        for i, o in enumerate(ops):
            if need[i]:
                c = cnt[o["eng"]]
                sig[i] = (o["eng"], c // SEM_ROT, c % SEM_ROT + 1)
                cnt[o["eng"]] = c + 1
        esem = {}
        for e in self.ENG:
            for k in range(cnt[e] // SEM_ROT + 1):
                esem[(e, k)] = nc.alloc_semaphore(f"s_{e}_{k}")
        dsem = {c: nc.alloc_semaphore(f"d_{c}") for c in self.dcount}
        for c, v in self.dcount.items():
            assert v * 16 < 60000, (c, v)
        self.n_sems = len(esem) + len(dsem)

        def run(engname):
            def body(e):
                waited = {}
                for i, o in enumerate(ops):
                    if o["eng"] != engname:
                        continue
                    waits = {}
                    for j in o["deps"]:
                        if engname == "pe" and ops[j]["eng"] == "pe":
                            continue
                        en, k, v = sig[j]
                        key = ("e", en, k)
                        waits[key] = max(waits.get(key, 0), v)
                    for c, v in o["ddeps"].items():
                        key = ("d", c)
                        waits[key] = max(waits.get(key, 0), 16 * v)
                    for key, v in waits.items():
                        if waited.get(key, 0) >= v:
                            continue
                        if key[0] == "e":
                            later = [kk for kk in waited if kk[0] == "e" and kk[1] == key[1] and kk[2] > key[2]]
                            if later:
                                continue
                            e.wait_ge(esem[(key[1], key[2])], v)
                        else:
                            e.wait_ge(dsem[key[1]], v)
                        waited[key] = v
                    if o["fn"] is None:
                        continue
                    ins = o["fn"](e)
                    if o["dma"] is not None:
                        ins.then_inc(dsem[o["dma"]], 16)
                    elif sig[i] is not None:
                        ins.then_inc(esem[(sig[i][0], sig[i][1])], 1)
            return body

        with nc.Block() as block:
            block.sync(run("sp"))
            block.scalar(run("act"))
            block.tensor(run("pe"))
            block.vector(run("dve"))
            block.gpsimd(run("pool"))


class Arena:
    def __init__(self, nc):
        self.nc = nc
        self.base = (nc.sbuf_base + 63) // 64 * 64
        self.top = nc.sbuf_top
        self.cur = self.base
        self.n = 0

    def alloc(self, name, shape, dt):
        sz = int(np.prod(shape[1:])) * (2 if dt == BF16 else 4)
        sz = (sz + 63) // 64 * 64
        off = self.cur
        assert off + sz <= self.top, (name, off, sz, self.top)
        self.cur += sz
        self.n += 1
        return self.nc.alloc_sbuf_tensor_at(f"{name}_{self.n}", list(shape), dt, offset=off)

    def mark(self):
        return self.cur

    def reset(self, m):
        self.cur = m


def build_nc(debug=False, phases=(1, 2, 3)):
    nc = bass.Bass("TRN2", target_bir_lowering=False)
    P = Prog()
    A = Arena(nc)

    def din(name, shape, dt=F32):
        return nc.dram_tensor(name, list(shape), dt, kind="ExternalInput").ap()

    x_d = din("x", [TOK, D])
    w_in_d = din("w_in", [D, 2048])
    w_out_d = din("w_out", [D, D])
    w_gate_d = din("w_gate", [D, DFF])
    w_up_d = din("w_up", [D, DFF])
    w_down_d = din("w_down", [DFF, D])
    w_pool_d = din("w_pool", [4, 128, 128])
    g1T_d = din("g1T", [128, 8])
    g2T_d = din("g2T", [128, 8])
    gq_d = din("gq", [128, 1])
    gk_d = din("gk", [128, 1])
    psc_d = din("pscale", [128, 4])
    tab_d = din("tab", [5, 128, 8, 640])
    ident_d = din("ident", [128, 128])
    bd_d = din("bd", [128, 128])
    invF_d = din("invF", [128, 4, 8])
    invL_d = din("invL", [128, 4, 8])
    out_d = nc.dram_tensor("out", [TOK, D], F32, kind="ExternalOutput").ap()

    skind = "ExternalOutput" if debug else "Internal"
    QT_d = nc.dram_tensor("QT", [4, 128, TOK], BF16, kind=skind).ap()
    KT_d = nc.dram_tensor("KT", [4, 128, TOK], BF16, kind=skind).ap()
    V_d = nc.dram_tensor("V", [TOK, 520], BF16, kind=skind).ap()
    UT_d = nc.dram_tensor("UT", [4, 128, TOK], F32, kind=skind).ap()
    X2_d = nc.dram_tensor("X2", [TOK, D], F32, kind=skind).ap()
    H2T_d = nc.dram_tensor("H2T", [8, 128, TOK], BF16, kind=skind).ap()

    PS = nc.alloc_psum_tensor("ps", [128, 4096], F32)

    def bank(k, n=1):
        return PS[:, 512 * k:512 * (k + n)]

    def bank_bf(k):
        return PS[:, 512 * k:512 * (k + 1)].bitcast(BF16)

    ident = A.alloc("ident", [128, 128], BF16)
    bd = A.alloc("bd", [128, 128], BF16)
    g1T = A.alloc("g1T", [128, 8], F32)
    g2T = A.alloc("g2T", [128, 8], F32)
    gq = A.alloc("gq", [128, 1], F32)
    gk = A.alloc("gk", [128, 1], F32)
    gk8 = A.alloc("gk8", [128, 1], F32)
    psc = A.alloc("psc", [128, 4], F32)
    P.op("pool", lambda e: e.dma_start(out=ident[:], in_=ident_d), w=["ident"], dma="c_ident")
    P.op("pool", lambda e: e.dma_start(out=bd[:], in_=bd_d), w=["bd"], dma="c_bd")
    P.op("sp", lambda e: e.dma_start(out=g1T[:], in_=g1T_d), w=["g1T"], dma="c_g1")
    P.op("sp", lambda e: e.dma_start(out=g2T[:], in_=g2T_d), w=["g2T"], dma="c_g2")
    P.op("sp", lambda e: e.dma_start(out=gq[:], in_=gq_d), w=["gq"], dma="c_gq")
    P.op("sp", lambda e: e.dma_start(out=gk[:], in_=gk_d), w=["gk"], dma="c_gk")
    P.op("sp", lambda e: e.dma_start(out=psc[:], in_=psc_d), w=["psc"], dma="c_psc")
    P.op("dve", lambda e: e.tensor_scalar(gk8[:], gk[:], 8.0, None, ALU.mult), r=["gk"], w=["gk8"])
    phase_mark = A.mark()

    def phase1():
        win = A.alloc("win", [128, 8, 2048], BF16)
        x1 = [A.alloc("x1", [128, 4, 1024], F32) for _ in range(2)]
        sqx = A.alloc("sqx", [128, 4, 1024], F32)
        ss = A.alloc("ss", [128, 4], F32)
        rt = A.alloc("rt", [128, 4], F32)
        rstd = A.alloc("rstd", [128, 4], F32)
        hb = [A.alloc("hb", [128, 1024], BF16) for _ in range(4)]
        hT = [A.alloc("hT", [128, 8, 512], BF16) for _ in range(2)]
        sq = [A.alloc("sq", [128, 512], BF16) for _ in range(2)]
        rs = [A.alloc("rs", [128, 512], F32) for _ in range(2)]
        rr = [A.alloc("rr", [128, 512], F32) for _ in range(2)]
        qst = [A.alloc("qst", [128, 4, 512], BF16) for _ in range(2)]
        kst = [A.alloc("kst", [128, 4, 512], BF16) for _ in range(2)]
        vst = [A.alloc("vst", [128, 4, 8, 65], BF16) for _ in range(2)]
        ust = [A.alloc("ust", [128, 4, 512], F32) for _ in range(2)]

        w_in_v = w_in_d.rearrange("(kc p) n -> p kc n", p=128)
        for g in range(4):
            P.op("pool", lambda e, g=g: e.dma_start(out=win[:, :, g * 512:(g + 1) * 512],
                                                    in_=w_in_v[:, :, g * 512:(g + 1) * 512]),
                 w=[("win", g)], dma=f"win{g}")
        for s in range(2):
            P.op("pool", lambda e, s=s: e.memset(vst[s][:], 1.0), w=[("vst", s)])

        def load_x(n):
            s = n % 2
            P.op("sp", lambda e: e.dma_start(
                out=x1[s][:], in_=x_d[n * 512:(n + 1) * 512, :].rearrange("(t p) d -> p t d", p=128)),
                w=[("x1", s, t) for t in range(4)], dma=f"x1_{s}")

        def norm_step(n, k):
            s = n % 2
            if k < 4:
                t = k
                P.op("act", lambda e: e.activation(out=sqx[:, t, :], in_=x1[s][:, t, :], func=AF.Square),
                     r=[("x1", s, t)], w=[("sqx", t)])
            elif k < 8:
                t = k - 4
                P.op("dve", lambda e: e.tensor_reduce(out=ss[:, t:t + 1], in_=sqx[:, t, :], axis=AX.X, op=ALU.add),
                     r=[("sqx", t)], w=[("ss", t)])
            elif k == 8:
                P.op("act", lambda e: e.activation(out=rt[:], in_=ss[:], func=AF.Ln, bias=EPS, scale=1.0 / D),
                     r=[("ss", t) for t in range(4)], w=["rt"])
            elif k == 9:
                P.op("act", lambda e: e.activation(out=rstd[:], in_=rt[:], func=AF.Exp, scale=-0.5), r=["rt"], w=["rstd"])
            elif k < 14:
                t = k - 10
                P.op("act", lambda e: e.activation(out=hb[t][:], in_=x1[s][:, t, :], func=AF.Copy,
                                                   scale=rstd[:, t:t + 1]),
                     r=[("x1", s, t), "rstd"], w=[("hb", t)])
            else:
                t = k - 14
                j = t % 2
                tp = bank_bf(j)
                for c in range(8):
                    P.op("pe", lambda e, c=c: e.transpose(tp[:, c * 128:(c + 1) * 128], hb[t][:, c * 128:(c + 1) * 128], ident[:]),
                         r=[("hb", t), "ident"], w=[("bank", j)])
                P.op("dve", lambda e: e.tensor_tensor(
                    out=hT[s][:, :, t * 128:(t + 1) * 128],
                    in0=tp.rearrange("p (c q) -> p c q", c=8),
                    in1=g1T[:, :].unsqueeze(2).to_broadcast([128, 8, 128]), op=ALU.mult),
                    r=[("bank", j), "g1T"], w=[("hT", s, t)])
        NSTEP = 18
        SCHED = {0: [0, 1], 1: [2, 4], 2: [3, 5], 3: [6], 4: [7], 5: [8], 6: [9], 7: [10], 8: [11], 9: [12, 14],
                 10: [13], 11: [15], 12: [16], 13: [17], 14: [], 15: []}

        mmb = [0]

        def next_bank():
            b = 2 + (mmb[0] % 5)
            mmb[0] += 1
            return b

        smb = [0]

        def qk_A(n, c):
            s = n % 2
            b = next_bank()
            pb = bank(b)
            for kc in range(8):
                P.op("pe", lambda e, kc=kc: e.matmul(pb, win[:, kc, c * 128:(c + 1) * 128], hT[s][:, kc, :],
                                                    start=(kc == 0), stop=(kc == 7)),
                     r=[("win", c // 4)] + [("hT", s, t) for t in range(4)], w=[("bank", b)])
            j = smb[0] % 2
            smb[0] += 1
            P.op("act", lambda e: e.activation(out=sq[j][:], in_=pb, func=AF.Square),
                 r=[("bank", b)], w=[("sq", j)])
            return (n, c, b, j)

        def qk_B(st_):
            n, c, b, j = st_
            s = n % 2
            pb = bank(b)
            isq = c < 4
            st = qst[s] if isq else kst[s]
            stname = "qst" if isq else "kst"
            b2 = 7
            pb2 = bank(b2)
            P.op("pe", lambda e: e.matmul(pb2, bd[:], sq[j][:], start=True, stop=True),
                 r=["bd", ("sq", j)], w=[("bank", b2)])
            P.op("act", lambda e: e.activation(out=rs[j][:], in_=pb2, func=AF.Ln, bias=64.0 * EPS, scale=1.0),
                 r=[("bank", b2)], w=[("rs", j)])
            P.op("act", lambda e: e.activation(out=rr[j][:], in_=rs[j][:], func=AF.Exp, scale=-0.5),
                 r=[("rs", j)], w=[("rr", j)])
            gv = gq if isq else gk8
            P.op("dve", lambda e: e.scalar_tensor_tensor(out=st[:, c % 4, :], in0=pb, scalar=gv[:, 0:1], in1=rr[j][:],
                                                         op0=ALU.mult, op1=ALU.mult),
                 r=[("bank", b), ("rr", j), "gq", "gk8"], w=[(stname, s, c % 4)])
            if c % 4 == 3:
                dst = QT_d if isq else KT_d
                P.op("sp", lambda e: e.dma_start(out=dst[:, :, n * 512:(n + 1) * 512].rearrange("c p t -> p c t"),
                                                 in_=st[:]),
                     r=[(stname, s, cc) for cc in range(4)], w=[("dram", stname, n)], dma=f"{stname}_{s}")

        def v_tile(n, t):
            s = n % 2
            b = next_bank()
            pb = bank(b)
            for kc in range(8):
                P.op("pe", lambda e, kc=kc: e.matmul(pb, hT[s][:, kc, t * 128:(t + 1) * 128], win[:, kc, 1024:1536],
                                                    start=(kc == 0), stop=(kc == 7)),
                     r=[("win", 2), ("hT", s, t)], w=[("bank", b)])
            P.op("dve", lambda e: e.tensor_copy(vst[s][:, t, :, 0:64], pb.rearrange("p (h d) -> p h d", h=8)),
                 r=[("bank", b)], w=[("vst", s)])
            if t == 3:
                P.op("sp", lambda e: e.dma_start(
                    out=V_d[n * 512:(n + 1) * 512, :].rearrange("(t p) e -> p t e", p=128),
                    in_=vst[s][:].rearrange("p t h d -> p t (h d)")),
                    r=[("vst", s)], w=[("dram", "v", n)], dma=f"vst_{s}")

        def u_chunk(n, g):
            s = n % 2
            b = next_bank()
            pb = bank(b)
            c = 12 + g
            for kc in range(8):
                P.op("pe", lambda e, kc=kc: e.matmul(pb, win[:, kc, c * 128:(c + 1) * 128], hT[s][:, kc, :],
                                                    start=(kc == 0), stop=(kc == 7)),
                     r=[("win", 3)] + [("hT", s, t) for t in range(4)], w=[("bank", b)])
            P.op("dve", lambda e: e.tensor_copy(ust[s][:, g, :], pb),
                 r=[("bank", b)], w=[("ust", s, g)])
            if g == 3:
                P.op("sp", lambda e: e.dma_start(out=UT_d[:, :, n * 512:(n + 1) * 512].rearrange("c p t -> p c t"),
                                                 in_=ust[s][:]),
                     r=[("ust", s, gg) for gg in range(4)], w=[("dram", "u", n)], dma=f"ust_{s}")

        load_x(0)
        load_x(1)
        for k in range(NSTEP):
            norm_step(0, k)
        units = [("qk", 0), ("v", 0), ("qk", 1), ("u", 0), ("qk", 2), ("v", 1), ("qk", 3), ("u", 1),
                 ("qk", 4), ("v", 2), ("qk", 5), ("u", 2), ("qk", 6), ("v", 3), ("qk", 7), ("u", 3)]
        for n in range(NBLK):
            pend = None
            if n + 2 < NBLK:
                load_x(n + 2)
            for i, (kind, a) in enumerate(units):
                if kind == "qk":
                    st_ = qk_A(n, a)
                    if pend is not None:
                        qk_B(pend)
                    pend = st_
                elif kind == "v":
                    v_tile(n, a)
                else:
                    u_chunk(n, a)
                if n + 1 < NBLK:
                    for k in SCHED[i]:
                        norm_step(n + 1, k)
            qk_B(pend)

    def phase2():
        wout = A.alloc("wout", [128, 8, 1024], BF16)
        wpool = A.alloc("wpool", [128, 4, 128], BF16)
        tabI = A.alloc("tabI", [128, 8, 640], BF16)
        tabBs = [None] + [A.alloc("tabB", [128, 8, 512], BF16) for _ in range(4)]
        invF = A.alloc("invF", [128, 4, 8], F32)
        invL = A.alloc("invL", [128, 4, 8], F32)
        KTb = [A.alloc("KTb", [128, 4, 1024], BF16) for _ in range(2)]
        Vb = [A.alloc("Vb", [128, 8, 8, 65], BF16) for _ in range(2)]
        QA = [A.alloc("QA", [128, 4, 512], BF16) for _ in range(2)]
        QB = [A.alloc("QB", [128, 4, 512], BF16) for _ in range(2)]
        Ub = [A.alloc("Ub", [128, 4, 528], F32) for _ in range(2)]
        T1 = A.alloc("T1", [128, 4, 528], F32)
        T2 = A.alloc("T2", [128, 3, 528], F32)
        tmp8 = A.alloc("tmp8", [128, 4, 8], F32)
        dT = A.alloc("dT", [128, 4, 512], BF16)
        xt = [A.alloc("xt", [128, 1024], F32) for _ in range(3)]
        mixT = A.alloc("mixT", [128, 8, 512], BF16)
        Pb = [A.alloc("Pb", [128, 5, 128], BF16) for _ in range(4)]
        rc = [A.alloc("rc", [128, 4], F32) for _ in range(2)]
        an = [A.alloc("an", [128, 512], BF16) for _ in range(2)]
        sqx = A.alloc("sqx2", [128, 1024], F32)
        ss2 = [A.alloc("ss2", [128, 1], F32) for _ in range(2)]
        rt2 = [A.alloc("rt2", [128, 1], F32) for _ in range(2)]
        rstd2 = [A.alloc("rstd2", [128, 1], F32) for _ in range(2)]
        h2b = [A.alloc("h2b", [128, 1024], BF16) for _ in range(2)]
        h2T = [A.alloc("h2T", [128, 8, 512], BF16) for _ in range(2)]

        Sps = [PS[:, 0:640], PS[:, 1024:1664]]
        Ops = [PS[:, 512 * 4:512 * 4 + 260].rearrange("p (h e) -> p h e", h=4),
               PS[:, 512 * 5:512 * 5 + 260].rearrange("p (h e) -> p h e", h=4)]
        gb = [0]

        wide = [False]

        def next_gbank():
            if wide[0]:
                b = 4 + (gb[0] % 4)
            else:
                b = 6 + (gb[0] % 2)
            gb[0] += 1
            return b

        P.op("pool", lambda e: e.dma_start(out=wout[:], in_=w_out_d.rearrange("(kc p) n -> p kc n", p=128)),
             w=["wout"], dma="wout")
        P.op("pool", lambda e: e.dma_start(out=wpool[:], in_=w_pool_d.rearrange("g c d -> c g d")),
             w=["wpool"], dma="wpool")
        P.op("pool", lambda e: e.dma_start(out=tabI[:], in_=tab_d[0]), w=["tabI"], dma="tabI")
        for v in range(1, 5):
            P.op("pool", lambda e, v=v: e.dma_start(out=tabBs[v][:], in_=tab_d[v][:, :, 0:512]),
                 w=[("tabB", v)], dma=f"tabB{v}")
        P.op("sp", lambda e: e.dma_start(out=invF[:], in_=invF_d), w=["invF"], dma="c_invF")
        P.op("sp", lambda e: e.dma_start(out=invL[:], in_=invL_d), w=["invL"], dma="c_invL")
        for s in range(2):
            P.op("pool", lambda e, s=s: e.memset(QA[s][:], 0.0), w=[("QA", s)])
            P.op("pool", lambda e, s=s: e.memset(QB[s][:], 0.0), w=[("QB", s)])

        blocks = [(sq_, b) for sq_ in range(2) for b in range(8)]

        def loads(idx):
            sq_, b = blocks[idx]
            s = idx % 2
            T0 = sq_ * SEQ
            ws = 512 * b - 256
            lo = max(0, ws)
            hi = min(SEQ, ws + 1024)
            P.op("sp", lambda e: e.dma_start(out=KTb[s][:, :, lo - ws:hi - ws],
                                             in_=KT_d[:, :, T0 + lo:T0 + hi].rearrange("c p t -> p c t")),
                 r=[("dram", "kst", n) for n in range(NBLK)], w=[("KTb", s)], dma=f"KTb_{s}")
            j0 = (lo - ws) // 128
            j1 = (hi - ws) // 128
            P.op("sp", lambda e: e.dma_start(
                out=Vb[s][:, j0:j1, :, :].rearrange("p j h e -> p j (h e)"),
                in_=V_d[T0 + lo:T0 + hi, :].rearrange("(j p) e -> p j e", p=128)),
                r=[("dram", "v", n) for n in range(NBLK)], w=[("Vb", s)], dma=f"Vb_{s}")
            q0 = T0 + 512 * b
            P.op("sp", lambda e: e.dma_start(out=QA[s][0:64, :, :],
                                             in_=QT_d[:, 0:64, q0:q0 + 512].rearrange("c p t -> p c t")),
                 r=[("dram", "qst", n) for n in range(NBLK)], w=[("QA", s)], dma=f"QA_{s}")
            P.op("sp", lambda e: e.dma_start(out=QB[s][64:128, :, :],
                                             in_=QT_d[:, 64:128, q0:q0 + 512].rearrange("c p t -> p c t")),
                 r=[("dram", "qst", n) for n in range(NBLK)], w=[("QB", s)], dma=f"QB_{s}")
            ulo = max(0, 512 * b - 8)
            uhi = min(SEQ, 512 * b + 520)
            i0 = ulo - (512 * b - 8)
            i1 = uhi - (512 * b - 8)
            if i0 > 0:
                P.op("pool", lambda e: e.memset(Ub[s][:, :, 0:i0], 0.0), w=[("Ub", s)])
            if i1 < 528:
                P.op("pool", lambda e: e.memset(Ub[s][:, :, i1:528], 0.0), w=[("Ub", s)])
            P.op("sp", lambda e: e.dma_start(out=Ub[s][:, :, i0:i1],
                                             in_=UT_d[:, :, T0 + ulo:T0 + uhi].rearrange("c p t -> p c t")),
                 r=[("dram", "u", n) for n in range(NBLK)], w=[("Ub", s)], dma=f"Ub_{s}")

        xcnt = [0]

        def pooling(idx):
            sq_, b = blocks[idx]
            s = idx % 2
            U = Ub[s]
            eng = "pool"
            P.op(eng, lambda e: e.tensor_tensor(out=T1[:, :, 1:528], in0=U[:, :, 0:527], in1=U[:, :, 1:528], op=ALU.add),
                 r=[("Ub", s)], w=["T1"])
            P.op(eng, lambda e: e.tensor_tensor(out=T2[:, 0:3, 2:527], in0=T1[:, 1:4, 1:526], in1=T1[:, 1:4, 3:528], op=ALU.add),
                 r=["T1"], w=["T2"])
            P.op(eng, lambda e: e.tensor_tensor(out=T1[:, 2:4, 4:525], in0=T2[:, 1:3, 2:523], in1=T2[:, 1:3, 6:527], op=ALU.add),
                 r=["T2"], w=["T1"])
            P.op(eng, lambda e: e.tensor_tensor(out=T2[:, 2, 8:521], in0=T1[:, 3, 4:517], in1=T1[:, 3, 12:525], op=ALU.add),
                 r=["T1"], w=["T2"])
            srcs = [T1[:, 0, :], T2[:, 0, :], T1[:, 2, :], T2[:, 2, :]]
            for g in range(4):
                w_ = 2 << g
                P.op("dve", lambda e, g=g, w_=w_: e.scalar_tensor_tensor(
                    out=dT[:, g, :], in0=srcs[g][:, 8:520], scalar=1.0 / w_, in1=U[:, g, 8:520],
                    op0=ALU.mult, op1=ALU.subtract),
                    r=["T1", "T2", ("Ub", s)], w=[("dT", g)])
            if b == 0 or b == 7:
                c0 = 0 if b == 0 else 504
                inv = invF if b == 0 else invL
                for g in range(4):
                    P.op("dve", lambda e, g=g: e.tensor_tensor(out=tmp8[:, g, :], in0=srcs[g][:, 8 + c0:16 + c0],
                                                               in1=inv[:, g, :], op=ALU.mult),
                         r=["T1", "T2", "invF", "invL"], w=[("tmp8", g)])
                    P.op("dve", lambda e, g=g: e.tensor_tensor(out=dT[:, g, c0:c0 + 8], in0=tmp8[:, g, :],
                                                               in1=U[:, g, 8 + c0:16 + c0], op=ALU.subtract),
                         r=[("tmp8", g), ("Ub", s)], w=[("dT", g)])
        def pooling_mm(idx):
            for g in range(4):
                bk = next_gbank()
                pb = bank(bk)
                P.op("pe", lambda e, g=g, pb=pb: e.matmul(pb, wpool[:, g, :], dT[:, g, :], start=True, stop=True),
                     r=["wpool", ("dT", g)], w=[("bank", bk)])
                P.op("dve", lambda e, g=g, pb=pb: e.tensor_scalar(mixT[:, 4 + g, :], pb, psc[:, g:g + 1], None, ALU.mult),
                     r=[("bank", bk), "psc"], w=[("mixT", 4 + g)])

        pend_tr = [None]
        pend_atr = []

        def flush_atr():
            for f in pend_atr:
                f()
            del pend_atr[:]

        def attention(idx):
            for jp in range(2):
                rowpair2(idx, 2 * jp, 2 * jp + 1)
                if jp == 0:
                    pooling_mm(idx)
            flush_atr()

        class RP:
            pass

        def rowpair2(idx, jA, jB):
            sq_, b = blocks[idx]
            s = idx % 2
            ws_tile = 4 * b - 2
            rps = []
            for slot, j in enumerate((jA, jB)):
                rp = RP()
                rp.slot = slot
                rp.j = j
                rp.r = 8 * b + 2 * j
                rp.ks = min(max(rp.r - 4, 0), 56)
                rp.nt = 5 if 4 <= rp.r <= 58 else 4
                var = {0: 1, 2: 2, 60: 3, 62: 4}.get(rp.r, 0)
                rp.tab = tabI if var == 0 else tabBs[var]
                rp.tabname = "tabI" if var == 0 else ("tabB", var)
                rp.W = rp.nt * 128
                rp.qs = slice(j * 128, (j + 1) * 128)
                rp.S = Sps[slot]
                rp.O = Ops[slot]
                rps.append(rp)

            def qk(rp, h):
                hp = h // 2
                Q = QA[s] if h % 2 == 0 else QB[s]
                Qn = "QA" if h % 2 == 0 else "QB"
                for t in range(rp.nt):
                    jt = rp.ks // 2 + t - ws_tile
                    P.op("pe", lambda e, t=t, jt=jt: e.matmul(rp.S[:, t * 128:(t + 1) * 128],
                                                            KTb[s][:, hp, jt * 128:(jt + 1) * 128],
                                                            Q[:, hp, rp.qs], start=True, stop=False),
                         r=[("KTb", s), (Qn, s)], w=[("S", rp.slot)])
                    P.op("pe", lambda e, t=t: e.matmul(rp.S[:, t * 128:(t + 1) * 128], ident[:],
                                                      rp.tab[:, h, t * 128:(t + 1) * 128], start=False, stop=True),
                         r=["ident", rp.tabname], w=[("S", rp.slot)])

            def softmax(rp, h):
                pbuf = Pb[rp.slot * 2 + h % 2]
                P.op("act", lambda e: e.activation(out=pbuf[:, 0:rp.nt, :].rearrange("p t q -> p (t q)"),
                                                   in_=rp.S[:, 0:rp.W], func=AF.Exp),
                     r=[("S", rp.slot)], w=[("Pb", rp.slot, h % 2)])

            def pv(rp, h):
                pbuf = Pb[rp.slot * 2 + h % 2]
                for t in range(rp.nt):
                    jt = rp.ks // 2 + t - ws_tile
                    P.op("pe", lambda e, t=t, jt=jt: e.matmul(rp.O[:, h % 4, :], pbuf[:, t, :], Vb[s][:, jt, h, :],
                                                            start=(t == 0), stop=(t == rp.nt - 1)),
                         r=[("Pb", rp.slot, h % 2), ("Vb", s)], w=[("bank", 4 + rp.slot)])

            def norm_half(rp, hh):
                rcb = rc[rp.slot]
                anb = an[rp.slot]
                P.op("dve", lambda e: e.reciprocal(rcb[:], rp.O[:, :, 64]),
                     r=[("bank", 4 + rp.slot)], w=[("rc", rp.slot)])
                P.op("dve", lambda e: e.tensor_tensor(
                    out=anb[:, hh * 256:(hh + 1) * 256].rearrange("p (h d) -> p h d", h=4),
                    in0=rp.O[:, :, 0:64],
                    in1=rcb[:, :].unsqueeze(2).to_broadcast([128, 4, 64]), op=ALU.mult),
                    r=[("bank", 4 + rp.slot), ("rc", rp.slot)], w=[("an", rp.slot, hh)])

            def make_atr(rp):
                def atr():
                    bk = next_gbank()
                    tp = bank_bf(bk)
                    anb = an[rp.slot]
                    for c in range(4):
                        P.op("pe", lambda e, c=c: e.transpose(tp[:, c * 128:(c + 1) * 128], anb[:, c * 128:(c + 1) * 128], ident[:]),
                             r=[("an", rp.slot, 0), ("an", rp.slot, 1), "ident"], w=[("bank", bk)])
                    P.op("dve", lambda e: e.tensor_copy(mixT[:, 0:4, rp.qs], tp[:, 0:512].rearrange("p (c q) -> p c q", c=4)),
                         r=[("bank", bk)], w=[("mixT", 0), ("mixT", 1), ("mixT", 2), ("mixT", 3)])
                return atr

            for h in range(9):
                if h < 8:
                    for rp in rps:
                        qk(rp, h)
                        softmax(rp, h)
                if h == 0:
                    flush_atr()
                if h == 3 and pend_tr[0] is not None:
                    pend_tr[0]()
                    pend_tr[0] = None
                if h >= 1:
                    for rp in rps:
                        pv(rp, h - 1)
                        if h - 1 == 3:
                            norm_half(rp, 0)
                        if h - 1 == 7:
                            norm_half(rp, 1)
            for rp in rps:
                pend_atr.append(make_atr(rp))

        def load_xt(idx, t):
            sq_, b = blocks[idx]
            k = xcnt[0] % 3
            xcnt[0] += 1
            tok = sq_ * SEQ + 512 * b + 128 * t
            P.op("sp", lambda e: e.dma_start(out=xt[k][:], in_=x_d[tok:tok + 128, :]), w=[("xt", k)], dma=f"xt_{k}")
            return k

        def wout_tile(idx, t, k):
            sq_, b = blocks[idx]
            s = idx % 2
            tok = sq_ * SEQ + 512 * b + 128 * t
            for hf in range(2):
                bk = next_gbank()
                pb = bank(bk)
                for kc in range(8):
                    P.op("pe", lambda e, kc=kc, pb=pb, hf=hf: e.matmul(pb, mixT[:, kc, t * 128:(t + 1) * 128],
                                                             wout[:, kc, hf * 512:(hf + 1) * 512],
                                                             start=(kc == 0), stop=(kc == 7)),
                         r=[("mixT", kc), "wout"], w=[("bank", bk)])
                P.op("dve", lambda e, pb=pb, hf=hf: e.tensor_tensor(out=xt[k][:, hf * 512:(hf + 1) * 512], in0=pb,
                                                                   in1=xt[k][:, hf * 512:(hf + 1) * 512], op=ALU.add),
                     r=[("bank", bk)], w=[("xt", k)])
            P.op("sp", lambda e: e.dma_start(out=X2_d[tok:tok + 128, :], in_=xt[k][:]),
                 r=[("xt", k)], w=[("dram", "x2", tok)], dma=f"xt_{k}")
            j = t % 2
            P.op("act", lambda e: e.activation(out=sqx[:], in_=xt[k][:], func=AF.Square), r=[("xt", k)], w=["sqx2"])
            P.op("dve", lambda e: e.tensor_reduce(out=ss2[j][:], in_=sqx[:], axis=AX.X, op=ALU.add),
                 r=["sqx2"], w=[("ss2", j)])
            P.op("act", lambda e: e.activation(out=rt2[j][:], in_=ss2[j][:], func=AF.Ln, bias=EPS, scale=1.0 / D),
                 r=[("ss2", j)], w=[("rt2", j)])
            P.op("act", lambda e: e.activation(out=rstd2[j][:], in_=rt2[j][:], func=AF.Exp, scale=-0.5),
                 r=[("rt2", j)], w=[("rstd2", j)])
            P.op("act", lambda e: e.activation(out=h2b[j][:], in_=xt[k][:], func=AF.Copy, scale=rstd2[j][:, 0:1]),
                 r=[("xt", k), ("rstd2", j)], w=[("h2b", j)])
            def deferred():
                bk = next_gbank()
                tp = bank_bf(bk)
                for c in range(8):
                    P.op("pe", lambda e, c=c: e.transpose(tp[:, c * 128:(c + 1) * 128], h2b[j][:, c * 128:(c + 1) * 128], ident[:]),
                         r=[("h2b", j), "ident"], w=[("bank", bk)])
                P.op("dve", lambda e: e.tensor_tensor(
                    out=h2T[s][:, :, t * 128:(t + 1) * 128], in0=tp.rearrange("p (c q) -> p c q", c=8),
                    in1=g2T[:, :].unsqueeze(2).to_broadcast([128, 8, 128]), op=ALU.mult),
                    r=[("bank", bk), "g2T"], w=[("h2T", s, t)])
                if t == 3:
                    n = idx
                    P.op("sp", lambda e: e.dma_start(out=H2T_d[:, :, n * 512:(n + 1) * 512].rearrange("c p t -> p c t"),
                                                     in_=h2T[s][:]),
                         r=[("h2T", s, tt) for tt in range(4)], w=[("dram", "h2T", n)], dma=f"h2T_{s}")
            return deferred

        loads(0)
        for idx in range(len(blocks)):
            if idx + 1 < len(blocks):
                loads(idx + 1)
            ks_ = [load_xt(idx, 0), load_xt(idx, 1)]
            if "pool" not in SKIP:
                pooling(idx)
            if "attn" not in SKIP:
                attention(idx)
            wide[0] = True
            for t in range(4):
                if t + 2 < 4:
                    ks_.append(load_xt(idx, t + 2))
                d_ = wout_tile(idx, t, ks_[t])
                if pend_tr[0] is not None:
                    pend_tr[0]()
                pend_tr[0] = d_
            wide[0] = False
        if pend_tr[0] is not None:
            pend_tr[0]()

    def phase3():
        wg = A.alloc("wg", [128, 8, DFF], BF16)
        wu = A.alloc("wu", [128, 8, DFF], BF16)
        wd = A.alloc("wd", [128, NFC, 1024], BF16)
        h2T = [A.alloc("h2Tb", [128, 8, 512], BF16) for _ in range(2)]
        x2t = [A.alloc("x2t", [128, 1024], F32) for _ in range(3)]
        actT = A.alloc("actT", [128, NFC, 512], BF16)
        sg = [A.alloc("sg", [128, 512], F32) for _ in range(2)]

        wg_v = w_gate_d.rearrange("(kc p) n -> p kc n", p=128)
        wu_v = w_up_d.rearrange("(kc p) n -> p kc n", p=128)
        wd_v = w_down_d.rearrange("(f p) n -> p f n", p=128)
        NG = 4
        cw = DFF // NG
        for g in range(NG):
            P.op("pool", lambda e, g=g: e.dma_start(out=wg[:, :, g * cw:(g + 1) * cw], in_=wg_v[:, :, g * cw:(g + 1) * cw]),
                 w=[("wg", g)], dma=f"wg{g}")
            P.op("pool", lambda e, g=g: e.dma_start(out=wu[:, :, g * cw:(g + 1) * cw], in_=wu_v[:, :, g * cw:(g + 1) * cw]),
                 w=[("wu", g)], dma=f"wu{g}")
        for g in range(2):
            P.op("pool", lambda e, g=g: e.dma_start(out=wd[:, g * 11:(g + 1) * 11, :], in_=wd_v[:, g * 11:(g + 1) * 11, :]),
                 w=[("wd", g)], dma=f"wd{g}")

        def wgrp(f):
            return sorted(set([(f * 128) // cw, (f * 128 + 127) // cw]))

        def load_h(n):
            s = n % 2
            P.op("sp", lambda e: e.dma_start(out=h2T[s][:], in_=H2T_d[:, :, n * 512:(n + 1) * 512].rearrange("c p t -> p c t")),
                 r=[("dram", "h2T", n)], w=[("h2Tb", s)], dma=f"h2Tb_{s}")

        xc = [0]

        def load_x2(n, t):
            k = xc[0] % 3
            xc[0] += 1
            tok = n * 512 + t * 128
            P.op("sp", lambda e: e.dma_start(out=x2t[k][:], in_=X2_d[tok:tok + 128, :]),
                 r=[("dram", "x2", tok)], w=[("x2t", k)], dma=f"x2t_{k}")
            return k

        gu = [0]
        dn = [0]
        load_h(0)
        for n in range(NBLK):
            s = n % 2
            if n + 1 < NBLK:
                load_h(n + 1)
            for f in range(NFC):
                i3 = gu[0] % 3
                gu[0] += 1
                bg = 2 * i3
                bu = 2 * i3 + 1
                pg = bank(bg)
                pu = bank(bu)
                for kc in range(8):
                    P.op("pe", lambda e, kc=kc, pg=pg, f=f, s=s: e.matmul(pg, wg[:, kc, f * 128:(f + 1) * 128], h2T[s][:, kc, :],
                                                                  start=(kc == 0), stop=(kc == 7)),
                         r=[("wg", g) for g in wgrp(f)] + [("h2Tb", s)], w=[("bank", bg)])
                for kc in range(8):
                    P.op("pe", lambda e, kc=kc, pu=pu, f=f, s=s: e.matmul(pu, wu[:, kc, f * 128:(f + 1) * 128], h2T[s][:, kc, :],
                                                                  start=(kc == 0), stop=(kc == 7)),
                         r=[("wu", g) for g in wgrp(f)] + [("h2Tb", s)], w=[("bank", bu)])
                j = f % 2
                P.op("act", lambda e, pg=pg, j=j: e.activation(out=sg[j][:], in_=pg, func=AF.Silu),
                     r=[("bank", bg)], w=[("sg", j)])
                P.op("dve", lambda e, pu=pu, j=j, f=f: e.tensor_tensor(out=actT[:, f, :], in0=pu, in1=sg[j][:], op=ALU.mult),
                     r=[("bank", bu), ("sg", j)], w=[("actT", f)])
            ks_ = [load_x2(n, 0), load_x2(n, 1)]
            for t in range(4):
                if t + 2 < 4:
                    ks_.append(load_x2(n, t + 2))
                k = ks_[t]
                for hf in range(2):
                    bk = 6 + (dn[0] % 2)
                    dn[0] += 1
                    pb = bank(bk)
                    for f in range(NFC):
                        P.op("pe", lambda e, f=f, pb=pb, hf=hf, t=t: e.matmul(pb, actT[:, f, t * 128:(t + 1) * 128],
                                                                             wd[:, f, hf * 512:(hf + 1) * 512],
                                                                             start=(f == 0), stop=(f == NFC - 1)),
                             r=[("actT", f), ("wd", f // 11)], w=[("bank", bk)])
                    P.op("dve", lambda e, pb=pb, hf=hf, k=k: e.tensor_tensor(out=x2t[k][:, hf * 512:(hf + 1) * 512], in0=pb,
                                                                             in1=x2t[k][:, hf * 512:(hf + 1) * 512], op=ALU.add),
                         r=[("bank", bk)], w=[("x2t", k)])
                tok = n * 512 + t * 128
                P.op("sp", lambda e, k=k, tok=tok: e.dma_start(out=out_d[tok:tok + 128, :], in_=x2t[k][:]),
                     r=[("x2t", k)], w=[("dram", "out", tok)], dma=f"x2t_{k}")

    if 1 in phases:
        phase1()
    P.barrier()
    A.reset(phase_mark)
    if 2 in phases:
        phase2()
    P.barrier()
    A.reset(phase_mark)
    if 3 in phases:
        phase3()
    P.final_wait("sp")
    P.emit(nc)
    return nc


def _tab_index():
    idx = np.zeros((5, 128, 640), dtype=np.int64)
    PAD = 15 * 31
    p = np.arange(128)
    qi = np.arange(128)
    for v, r in enumerate([8, 0, 2, 60, 62]):
        ks = min(max(r - 4, 0), 56)
        nt = 5 if 4 <= r <= 58 else 4
        for t in range(5):
            kr = ks + 2 * t + p[:, None] // 64
            kc = p[:, None] % 64
            rq = r + qi[None, :] // 64
            qc = qi[None, :] % 64
            rs = np.clip(rq - 4, 0, 56)
            cs = np.clip(qc - 8, 0, 48)
            valid = (kr >= rs) & (kr < rs + 8) & (kc >= cs) & (kc < cs + 16) & (t < nt)
            ii = (kr - rq + 7) * 31 + (kc - qc + 15)
            idx[v, :, t * 128:(t + 1) * 128] = np.where(valid, ii, PAD)
    return idx


def _inv_tables():
    invF = np.zeros((4, 8), np.float32)
    invL = np.zeros((4, 8), np.float32)
    for g, w in enumerate((2, 4, 8, 16)):
        for j in range(8):
            t = j
            cnt = min(t + w // 2, SEQ) - max(t - w // 2, 0)
            invF[g, j] = 1.0 / cnt
            t = SEQ - 8 + j
            cnt = min(t + w // 2, SEQ) - max(t - w // 2, 0)
            invL[g, j] = 1.0 / cnt
    return (np.ascontiguousarray(np.broadcast_to(invF[None], (128, 4, 8))),
            np.ascontiguousarray(np.broadcast_to(invL[None], (128, 4, 8))))


def make_in_maps(x, norm1_g, w_in, q_norm_g, k_norm_g, rpb, w_pool, pool_scale, w_out,
                 norm2_g, w_gate, w_up, w_down):
    f = lambda a: np.ascontiguousarray(np.asarray(a, dtype=np.float32))
    x = f(x).reshape(N_CORES, TOK, D)
    rpb_pad = np.concatenate([f(rpb)[0].reshape(8, 15 * 31), np.full((8, 1), NEG, np.float32)], axis=1)
    idx = _tab_index()
    tab = np.ascontiguousarray(rpb_pad[:, idx].transpose(1, 2, 0, 3))
    invF, invL = _inv_tables()
    bdm = np.zeros((128, 128), np.float32)
    bdm[:64, :64] = 1.0
    bdm[64:, 64:] = 1.0
    common = dict(
        w_in=f(w_in)[0], w_out=f(w_out)[0], w_gate=f(w_gate)[0], w_up=f(w_up)[0], w_down=f(w_down)[0],
        w_pool=f(w_pool)[0],
        g1T=np.ascontiguousarray(f(norm1_g)[0].reshape(8, 128).T),
        g2T=np.ascontiguousarray(f(norm2_g)[0].reshape(8, 128).T),
        gq=np.ascontiguousarray(np.tile(f(q_norm_g)[0], 2).reshape(128, 1)),
        gk=np.ascontiguousarray(np.tile(f(k_norm_g)[0], 2).reshape(128, 1)),
        pscale=np.ascontiguousarray(f(pool_scale)[0].reshape(4, 128).T),
        tab=tab, ident=np.eye(128, dtype=np.float32), bd=bdm, invF=invF, invL=invL,
    )
    return [dict(common, x=np.ascontiguousarray(x[c])) for c in range(N_CORES)]


_NC_CACHE = {}


def kernel(x, norm1_g, w_in, q_norm_g, k_norm_g, rpb, w_pool, pool_scale, w_out,
           norm2_g, w_gate, w_up, w_down):
    in_maps = make_in_maps(x, norm1_g, w_in, q_norm_g, k_norm_g, rpb, w_pool, pool_scale, w_out,
                           norm2_g, w_gate, w_up, w_down)
    if "nc" not in _NC_CACHE:
        _NC_CACHE["nc"] = build_nc()
    nc = _NC_CACHE["nc"]
    res = run_bass_kernel_spmd(nc, in_maps, core_ids=list(range(N_CORES)))
    out = np.stack([np.asarray(r["out"], dtype=np.float32) for r in res.results], axis=0)
    return out.reshape(16, SEQ, D)
```

```python
import numpy as np
import concourse.bass as bass
import concourse.mybir as mybir
from concourse.bass_utils import run_bass_kernel_spmd

F32 = mybir.dt.float32
BF16 = mybir.dt.bfloat16
ALU = mybir.AluOpType
AF = mybir.ActivationFunctionType
AX = mybir.AxisListType

N_CORES = 8
TOK = 8192
SEQ = 4096
D = 1024
NBLK = 16
DFF = 2816
NFC = 22
EPS = 1e-6
NEG = -30000.0
SEM_ROT = 30000
import os
SKIP = set(os.environ.get('KSKIP', '').split(','))


class Prog:
    ENG = ("sp", "act", "pe", "dve", "pool")

    def __init__(self):
        self.ops = []
        self.tiles = {}
        self.dcount = {}
        self.last = {e: None for e in self.ENG}
        self.bar = {e: None for e in self.ENG}

    def op(self, eng, fn, r=(), w=(), dma=None):
        i = len(self.ops)
        deps = set()
        ddeps = {}

        def add(j):
            o = self.ops[j]
            if o["dma"] is not None:
                c = o["dma"]
                ddeps[c] = max(ddeps.get(c, 0), self.dcount[c])
            else:
                deps.add(j)

        for t in r:
            st = self.tiles.setdefault(t, [None, []])
            if st[0] is not None:
                add(st[0])
        for t in w:
            st = self.tiles.setdefault(t, [None, []])
            if st[0] is not None:
                add(st[0])
            for j in st[1]:
                add(j)
        for t in r:
            self.tiles[t][1].append(i)
        for t in w:
            st = self.tiles[t]
            st[0] = i
            st[1] = []
        if self.bar[eng] is not None:
            bd, bdd = self.bar[eng]
            deps |= bd
            for c, v in bdd.items():
                ddeps[c] = max(ddeps.get(c, 0), v)
            self.bar[eng] = None
        latest = {}
        for j in deps:
            en = self.ops[j]["eng"]
            if en not in latest or j > latest[en]:
                latest[en] = j
        deps = set(latest.values())
        if dma is not None:
            self.dcount[dma] = self.dcount.get(dma, 0) + 1
        self.ops.append(dict(eng=eng, fn=fn, deps=deps, ddeps=ddeps, dma=dma))
        self.last[eng] = i
        return i

    def barrier(self):
        bd = set(j for j in self.last.values() if j is not None and self.ops[j]["dma"] is None)
        for e in self.ENG:
            for j in range(len(self.ops) - 1, -1, -1):
                if self.ops[j]["eng"] == e and self.ops[j]["dma"] is None and self.ops[j]["fn"] is not None:
                    bd.add(j)
                    break
        bdd = dict(self.dcount)
        for e in self.ENG:
            self.bar[e] = (set(bd), dict(bdd))

    def final_wait(self, eng="sp"):
        self.bar[eng] = (set(), dict(self.dcount))
        self.op(eng, None)

    def emit(self, nc):
        ops = self.ops
        need = [False] * len(ops)
        for o in ops:
            for j in o["deps"]:
                if not (o["eng"] == "pe" and ops[j]["eng"] == "pe"):
                    need[j] = True
        cnt = {e: 0 for e in self.ENG}
        sig = [None] * len(ops)
        for i, o in enumerate(ops):
            if need[i]:
                c = cnt[o["eng"]]
                sig[i] = (o["eng"], c // SEM_ROT, c % SEM_ROT + 1)
                cnt[o["eng"]] = c + 1
        esem = {}
        for e in self.ENG:
            for k in range(cnt[e] // SEM_ROT + 1):
                esem[(e, k)] = nc.alloc_semaphore(f"s_{e}_{k}")
        dsem = {c: nc.alloc_semaphore(f"d_{c}") for c in self.dcount}
        for c, v in self.dcount.items():
            assert v * 16 < 60000, (c, v)
        self.n_sems = len(esem) + len(dsem)

        def run(engname):
            def body(e):
                waited = {}
                for i, o in enumerate(ops):
                    if o["eng"] != engname:
                        continue
                    waits = {}
                    for j in o["deps"]:
                        if engname == "pe" and ops[j]["eng"] == "pe":
                            continue
                        en, k, v = sig[j]
                        key = ("e", en, k)
                        waits[key] = max(waits.get(key, 0), v)
                    for c, v in o["ddeps"].items():
                        key = ("d", c)
                        waits[key] = max(waits.get(key, 0), 16 * v)
                    for key, v in waits.items():
                        if waited.get(key, 0) >= v:
                            continue
                        if key[0] == "e":
                            later = [kk for kk in waited if kk[0] == "e" and kk[1] == key[1] and kk[2] > key[2]]
                            if later:
                                continue
                            e.wait_ge(esem[(key[1], key[2])], v)
                        else:
                            e.wait_ge(dsem[key[1]], v)
                        waited[key] = v
                    if o["fn"] is None:
                        continue
                    ins = o["fn"](e)
                    if o["dma"] is not None:
                        ins.then_inc(dsem[o["dma"]], 16)
                    elif sig[i] is not None:
                        ins.then_inc(esem[(sig[i][0], sig[i][1])], 1)
            return body

        with nc.Block() as block:
            block.sync(run("sp"))
            block.scalar(run("act"))
            block.tensor(run("pe"))
            block.vector(run("dve"))
            block.gpsimd(run("pool"))


class Arena:
    def __init__(self, nc):
        self.nc = nc
        self.base = (nc.sbuf_base + 63) // 64 * 64
        self.top = nc.sbuf_top
        self.cur = self.base
        self.n = 0

    def alloc(self, name, shape, dt):
        sz = int(np.prod(shape[1:])) * (2 if dt == BF16 else 4)
        sz = (sz + 63) // 64 * 64
        off = self.cur
        assert off + sz <= self.top, (name, off, sz, self.top)
        self.cur += sz
        self.n += 1
        return self.nc.alloc_sbuf_tensor_at(f"{name}_{self.n}", list(shape), dt, offset=off)

    def alloc_top(self, name, shape, dt):
        sz = int(np.prod(shape[1:])) * (2 if dt == BF16 else 4)
        sz = (sz + 63) // 64 * 64
        self.top -= sz
        self.top = self.top // 64 * 64
        assert self.top >= self.cur, (name, self.top, self.cur)
        self.n += 1
        return self.nc.alloc_sbuf_tensor_at(f"{name}_{self.n}", list(shape), dt, offset=self.top)

    def mark(self):
        return self.cur

    def reset(self, m):
        self.cur = m


def build_nc(debug=False, phases=(1, 2, 3)):
    nc = bass.Bass("TRN2", target_bir_lowering=False)
    P = Prog()
    A = Arena(nc)

    def din(name, shape, dt=F32):
        return nc.dram_tensor(name, list(shape), dt, kind="ExternalInput").ap()

    x_d = din("x", [TOK, D])
    w_in_d = din("w_in", [D, 2048])
    w_out_d = din("w_out", [D, D])
    w_gate_d = din("w_gate", [D, DFF])
    w_up_d = din("w_up", [D, DFF])
    w_down_d = din("w_down", [DFF, D])
    w_pool_d = din("w_pool", [4, 128, 128])
    g1T_d = din("g1T", [128, 8])
    g2T_d = din("g2T", [128, 8])
    gq_d = din("gq", [128, 1])
    gk_d = din("gk", [128, 1])
    psc_d = din("pscale", [128, 4])
    tab_d = din("tab", [5, 128, 8, 640])
    ident_d = din("ident", [128, 128])
    bd_d = din("bd", [128, 128])
    invF_d = din("invF", [128, 4, 8])
    invL_d = din("invL", [128, 4, 8])
    out_d = nc.dram_tensor("out", [TOK, D], F32, kind="ExternalOutput").ap()

    skind = "ExternalOutput" if debug else "Internal"
    QT_d = nc.dram_tensor("QT", [4, 128, TOK], BF16, kind=skind).ap()
    KT_d = nc.dram_tensor("KT", [4, 128, TOK], BF16, kind=skind).ap()
    V_d = nc.dram_tensor("V", [TOK, 520], BF16, kind=skind).ap()
    UT_d = nc.dram_tensor("UT", [4, 128, TOK], F32, kind=skind).ap()
    X2_d = nc.dram_tensor("X2", [TOK, D], F32, kind=skind).ap()
    H2T_d = nc.dram_tensor("H2T", [8, 128, TOK], BF16, kind=skind).ap()

    PS = nc.alloc_psum_tensor("ps", [128, 4096], F32)

    def bank(k, n=1):
        return PS[:, 512 * k:512 * (k + n)]

    def bank_bf(k):
        return PS[:, 512 * k:512 * (k + 1)].bitcast(BF16)

    ident = A.alloc("ident", [128, 128], BF16)
    bd = A.alloc("bd", [128, 128], BF16)
    g1T = A.alloc("g1T", [128, 8], F32)
    g2T = A.alloc("g2T", [128, 8], F32)
    gq = A.alloc("gq", [128, 1], F32)
    gk = A.alloc("gk", [128, 1], F32)
    gk8 = A.alloc("gk8", [128, 1], F32)
    psc = A.alloc("psc", [128, 4], F32)
    P.op("pool", lambda e: e.dma_start(out=ident[:], in_=ident_d), w=["ident"], dma="c_ident")
    P.op("pool", lambda e: e.dma_start(out=bd[:], in_=bd_d), w=["bd"], dma="c_bd")
    P.op("sp", lambda e: e.dma_start(out=g1T[:], in_=g1T_d), w=["g1T"], dma="c_g1")
    P.op("sp", lambda e: e.dma_start(out=g2T[:], in_=g2T_d), w=["g2T"], dma="c_g2")
    P.op("sp", lambda e: e.dma_start(out=gq[:], in_=gq_d), w=["gq"], dma="c_gq")
    P.op("sp", lambda e: e.dma_start(out=gk[:], in_=gk_d), w=["gk"], dma="c_gk")
    P.op("sp", lambda e: e.dma_start(out=psc[:], in_=psc_d), w=["psc"], dma="c_psc")
    P.op("dve", lambda e: e.tensor_scalar(gk8[:], gk[:], 8.0, None, ALU.mult), r=["gk"], w=["gk8"])
    phase_mark = A.mark()
    top_save = A.top
    tabI = A.alloc_top("tabI", [128, 8, 640], BF16)
    tabBs = [None] + [A.alloc_top("tabB", [128, 8, 512], BF16) for _ in range(4)]
    wpool = A.alloc_top("wpool", [128, 4, 128], BF16)
    top_p12 = A.top

    def early_phase2_loads():
        for v in (1, 2):
            P.op("pool", lambda e, v=v: e.dma_start(out=tabBs[v][:], in_=tab_d[v][:, :, 0:512]),
                 w=[("tabB", v)], dma=f"tabB{v}")
        P.op("pool", lambda e: e.dma_start(out=tabI[:], in_=tab_d[0]), w=["tabI"], dma="tabI")
        P.op("pool", lambda e: e.dma_start(out=wpool[:], in_=w_pool_d.rearrange("g c d -> c g d")),
             w=["wpool"], dma="wpool")
        for v in (3, 4):
            P.op("pool", lambda e, v=v: e.dma_start(out=tabBs[v][:], in_=tab_d[v][:, :, 0:512]),
                 w=[("tabB", v)], dma=f"tabB{v}")

    def phase1():
        win = A.alloc("win", [128, 8, 2048], BF16)
        x1 = [A.alloc("x1", [128, 4, 1024], F32) for _ in range(2)]
        sqx = A.alloc("sqx", [128, 4, 1024], F32)
        ss = A.alloc("ss", [128, 4], F32)
        rt = A.alloc("rt", [128, 4], F32)
        rstd = A.alloc("rstd", [128, 4], F32)
        hb = [A.alloc("hb", [128, 1024], BF16) for _ in range(4)]
        hT = [A.alloc("hT", [128, 8, 512], BF16) for _ in range(2)]
        sq = [A.alloc("sq", [128, 512], BF16) for _ in range(2)]
        rs = [A.alloc("rs", [128, 512], F32) for _ in range(2)]
        rr = [A.alloc("rr", [128, 512], F32) for _ in range(2)]
        qst = [A.alloc("qst", [128, 4, 512], BF16) for _ in range(2)]
        kst = [A.alloc("kst", [128, 4, 512], BF16) for _ in range(2)]
        vst = [A.alloc("vst", [128, 4, 8, 65], BF16) for _ in range(2)]
        ust = [A.alloc("ust", [128, 4, 512], F32) for _ in range(2)]

        w_in_v = w_in_d.rearrange("(kc p) n -> p kc n", p=128)
        for g in range(4):
            P.op("pool", lambda e, g=g: e.dma_start(out=win[:, :, g * 512:(g + 1) * 512],
                                                    in_=w_in_v[:, :, g * 512:(g + 1) * 512]),
                 w=[("win", g)], dma=f"win{g}")
        for s in range(2):
            P.op("pool", lambda e, s=s: e.memset(vst[s][:], 1.0), w=[("vst", s)])
        early_phase2_loads()

        def load_x(n):
            s = n % 2
            P.op("sp", lambda e: e.dma_start(
                out=x1[s][:], in_=x_d[n * 512:(n + 1) * 512, :].rearrange("(t p) d -> p t d", p=128)),
                w=[("x1", s, t) for t in range(4)], dma=f"x1_{s}")

        def norm_step(n, k):
            s = n % 2
            if k < 4:
                t = k
                P.op("act", lambda e: e.activation(out=sqx[:, t, :], in_=x1[s][:, t, :], func=AF.Square),
                     r=[("x1", s, t)], w=[("sqx", t)])
            elif k < 8:
                t = k - 4
                P.op("dve", lambda e: e.tensor_reduce(out=ss[:, t:t + 1], in_=sqx[:, t, :], axis=AX.X, op=ALU.add),
                     r=[("sqx", t)], w=[("ss", t)])
            elif k == 8:
                P.op("act", lambda e: e.activation(out=rt[:], in_=ss[:], func=AF.Ln, bias=EPS, scale=1.0 / D),
                     r=[("ss", t) for t in range(4)], w=["rt"])
            elif k == 9:
                P.op("act", lambda e: e.activation(out=rstd[:], in_=rt[:], func=AF.Exp, scale=-0.5), r=["rt"], w=["rstd"])
            elif k < 14:
                t = k - 10
                P.op("act", lambda e: e.activation(out=hb[t][:], in_=x1[s][:, t, :], func=AF.Copy,
                                                   scale=rstd[:, t:t + 1]),
                     r=[("x1", s, t), "rstd"], w=[("hb", t)])
            else:
                t = k - 14
                j = t % 2
                tp = bank_bf(j)
                for c in range(8):
                    P.op("pe", lambda e, c=c: e.transpose(tp[:, c * 128:(c + 1) * 128], hb[t][:, c * 128:(c + 1) * 128], ident[:]),
                         r=[("hb", t), "ident"], w=[("bank", j)])
                P.op("dve", lambda e: e.tensor_tensor(
                    out=hT[s][:, :, t * 128:(t + 1) * 128],
                    in0=tp.rearrange("p (c q) -> p c q", c=8),
                    in1=g1T[:, :].unsqueeze(2).to_broadcast([128, 8, 128]), op=ALU.mult),
                    r=[("bank", j), "g1T"], w=[("hT", s, t)])
        NSTEP = 18
        SCHED = {0: [0, 1], 1: [2, 4], 2: [3, 5], 3: [6], 4: [7], 5: [8], 6: [9], 7: [10], 8: [11], 9: [12, 14],
                 10: [13], 11: [15], 12: [16], 13: [17], 14: [], 15: []}

        mmb = [0]

        def next_bank():
            b = 2 + (mmb[0] % 5)
            mmb[0] += 1
            return b

        smb = [0]

        def qk_A(n, c):
            s = n % 2
            b = next_bank()
            pb = bank(b)
            for kc in range(8):
                P.op("pe", lambda e, kc=kc: e.matmul(pb, win[:, kc, c * 128:(c + 1) * 128], hT[s][:, kc, :],
                                                    start=(kc == 0), stop=(kc == 7)),
                     r=[("win", c // 4)] + [("hT", s, t) for t in range(4)], w=[("bank", b)])
            j = smb[0] % 2
            smb[0] += 1
            P.op("act", lambda e: e.activation(out=sq[j][:], in_=pb, func=AF.Square),
                 r=[("bank", b)], w=[("sq", j)])
            return (n, c, b, j)

        def qk_B(st_):
            n, c, b, j = st_
            s = n % 2
            pb = bank(b)
            isq = c < 4
            st = qst[s] if isq else kst[s]
            stname = "qst" if isq else "kst"
            b2 = 7
            pb2 = bank(b2)
            P.op("pe", lambda e: e.matmul(pb2, bd[:], sq[j][:], start=True, stop=True),
                 r=["bd", ("sq", j)], w=[("bank", b2)])
            P.op("act", lambda e: e.activation(out=rs[j][:], in_=pb2, func=AF.Ln, bias=64.0 * EPS, scale=1.0),
                 r=[("bank", b2)], w=[("rs", j)])
            P.op("act", lambda e: e.activation(out=rr[j][:], in_=rs[j][:], func=AF.Exp, scale=-0.5),
                 r=[("rs", j)], w=[("rr", j)])
            gv = gq if isq else gk8
            P.op("dve", lambda e: e.scalar_tensor_tensor(out=st[:, c % 4, :], in0=pb, scalar=gv[:, 0:1], in1=rr[j][:],
                                                         op0=ALU.mult, op1=ALU.mult),
                 r=[("bank", b), ("rr", j), "gq", "gk8"], w=[(stname, s, c % 4)])
            if c % 4 == 3:
                dst = QT_d if isq else KT_d
                P.op("sp", lambda e: e.dma_start(out=dst[:, :, n * 512:(n + 1) * 512].rearrange("c p t -> p c t"),
                                                 in_=st[:]),
                     r=[(stname, s, cc) for cc in range(4)], w=[("dram", stname, n)], dma=f"{stname}_{s}")

        def v_tile(n, t):
            s = n % 2
            b = next_bank()
            pb = bank(b)
            for kc in range(8):
                P.op("pe", lambda e, kc=kc: e.matmul(pb, hT[s][:, kc, t * 128:(t + 1) * 128], win[:, kc, 1024:1536],
                                                    start=(kc == 0), stop=(kc == 7)),
                     r=[("win", 2), ("hT", s, t)], w=[("bank", b)])
            P.op("dve", lambda e: e.tensor_copy(vst[s][:, t, :, 0:64], pb.rearrange("p (h d) -> p h d", h=8)),
                 r=[("bank", b)], w=[("vst", s)])
            if t == 3:
                P.op("sp", lambda e: e.dma_start(
                    out=V_d[n * 512:(n + 1) * 512, :].rearrange("(t p) e -> p t e", p=128),
                    in_=vst[s][:].rearrange("p t h d -> p t (h d)")),
                    r=[("vst", s)], w=[("dram", "v", n)], dma=f"vst_{s}")

        def u_chunk(n, g):
            s = n % 2
            b = next_bank()
            pb = bank(b)
            c = 12 + g
            for kc in range(8):
                P.op("pe", lambda e, kc=kc: e.matmul(pb, win[:, kc, c * 128:(c + 1) * 128], hT[s][:, kc, :],
                                                    start=(kc == 0), stop=(kc == 7)),
                     r=[("win", 3)] + [("hT", s, t) for t in range(4)], w=[("bank", b)])
            P.op("dve", lambda e: e.tensor_copy(ust[s][:, g, :], pb),
                 r=[("bank", b)], w=[("ust", s, g)])
            if g == 3:
                P.op("sp", lambda e: e.dma_start(out=UT_d[:, :, n * 512:(n + 1) * 512].rearrange("c p t -> p c t"),
                                                 in_=ust[s][:]),
                     r=[("ust", s, gg) for gg in range(4)], w=[("dram", "u", n)], dma=f"ust_{s}")

        load_x(0)
        load_x(1)
        for k in range(NSTEP):
            norm_step(0, k)
        units = [("qk", 0), ("v", 0), ("qk", 1), ("u", 0), ("qk", 2), ("v", 1), ("qk", 3), ("u", 1),
                 ("qk", 4), ("v", 2), ("qk", 5), ("u", 2), ("qk", 6), ("v", 3), ("qk", 7), ("u", 3)]
        for n in range(NBLK):
            pend = None
            if n + 2 < NBLK:
                load_x(n + 2)
            for i, (kind, a) in enumerate(units):
                if kind == "qk":
                    st_ = qk_A(n, a)
                    if pend is not None:
                        qk_B(pend)
                    pend = st_
                elif kind == "v":
                    v_tile(n, a)
                else:
                    u_chunk(n, a)
                if n + 1 < NBLK:
                    for k in SCHED[i]:
                        norm_step(n + 1, k)
            qk_B(pend)

    def phase2():
        wout = A.alloc("wout", [128, 8, 1024], BF16)
        invF = A.alloc("invF", [128, 4, 8], F32)
        invL = A.alloc("invL", [128, 4, 8], F32)
        KTb = [A.alloc("KTb", [128, 4, 1024], BF16) for _ in range(2)]
        Vb = [A.alloc("Vb", [128, 8, 8, 65], BF16) for _ in range(2)]
        QA = [A.alloc("QA", [128, 4, 512], BF16) for _ in range(2)]
        QB = [A.alloc("QB", [128, 4, 512], BF16) for _ in range(2)]
        Ub = [A.alloc("Ub", [128, 4, 528], F32) for _ in range(2)]
        T1 = A.alloc("T1", [128, 4, 528], F32)
        T2 = A.alloc("T2", [128, 3, 528], F32)
        tmp8 = A.alloc("tmp8", [128, 4, 8], F32)
        dT = A.alloc("dT", [128, 4, 512], BF16)
        xt = [A.alloc("xt", [128, 1024], F32) for _ in range(3)]
        mixT = A.alloc("mixT", [128, 8, 512], BF16)
        Pb = [A.alloc("Pb", [128, 5, 128], BF16) for _ in range(4)]
        rc = [A.alloc("rc", [128, 4], F32) for _ in range(2)]
        an = [A.alloc("an", [128, 512], BF16) for _ in range(2)]
        sqx = A.alloc("sqx2", [128, 1024], F32)
        ss2 = [A.alloc("ss2", [128, 1], F32) for _ in range(2)]
        rt2 = [A.alloc("rt2", [128, 1], F32) for _ in range(2)]
        rstd2 = [A.alloc("rstd2", [128, 1], F32) for _ in range(2)]
        h2b = [A.alloc("h2b", [128, 1024], BF16) for _ in range(2)]
        h2T = [A.alloc("h2T", [128, 8, 512], BF16) for _ in range(2)]

        Sps = [PS[:, 0:640], PS[:, 1024:1664]]
        Ops = [PS[:, 512 * 4:512 * 4 + 260].rearrange("p (h e) -> p h e", h=4),
               PS[:, 512 * 5:512 * 5 + 260].rearrange("p (h e) -> p h e", h=4)]
        gb = [0]

        wide = [False]

        def next_gbank():
            if wide[0]:
                b = 4 + (gb[0] % 4)
            else:
                b = 6 + (gb[0] % 2)
            gb[0] += 1
            return b

        P.op("pool", lambda e: e.dma_start(out=wout[:], in_=w_out_d.rearrange("(kc p) n -> p kc n", p=128)),
             w=["wout"], dma="wout")
        P.op("sp", lambda e: e.dma_start(out=invF[:], in_=invF_d), w=["invF"], dma="c_invF")
        P.op("sp", lambda e: e.dma_start(out=invL[:], in_=invL_d), w=["invL"], dma="c_invL")
        for s in range(2):
            P.op("pool", lambda e, s=s: e.memset(QA[s][:], 0.0), w=[("QA", s)])
            P.op("pool", lambda e, s=s: e.memset(QB[s][:], 0.0), w=[("QB", s)])

        blocks = [(sq_, b) for sq_ in range(2) for b in range(8)]

        def loads(idx):
            sq_, b = blocks[idx]
            s = idx % 2
            T0 = sq_ * SEQ
            ws = 512 * b - 256
            lo = max(0, ws)
            hi = min(SEQ, ws + 1024)
            P.op("sp", lambda e: e.dma_start(out=KTb[s][:, :, lo - ws:hi - ws],
                                             in_=KT_d[:, :, T0 + lo:T0 + hi].rearrange("c p t -> p c t")),
                 r=[("dram", "kst", n) for n in range(NBLK)], w=[("KTb", s)], dma=f"KTb_{s}")
            j0 = (lo - ws) // 128
            j1 = (hi - ws) // 128
            P.op("sp", lambda e: e.dma_start(
                out=Vb[s][:, j0:j1, :, :].rearrange("p j h e -> p j (h e)"),
                in_=V_d[T0 + lo:T0 + hi, :].rearrange("(j p) e -> p j e", p=128)),
                r=[("dram", "v", n) for n in range(NBLK)], w=[("Vb", s)], dma=f"Vb_{s}")
            q0 = T0 + 512 * b
            P.op("sp", lambda e: e.dma_start(out=QA[s][0:64, :, :],
                                             in_=QT_d[:, 0:64, q0:q0 + 512].rearrange("c p t -> p c t")),
                 r=[("dram", "qst", n) for n in range(NBLK)], w=[("QA", s)], dma=f"QA_{s}")
            P.op("sp", lambda e: e.dma_start(out=QB[s][64:128, :, :],
                                             in_=QT_d[:, 64:128, q0:q0 + 512].rearrange("c p t -> p c t")),
                 r=[("dram", "qst", n) for n in range(NBLK)], w=[("QB", s)], dma=f"QB_{s}")
            ulo = max(0, 512 * b - 8)
            uhi = min(SEQ, 512 * b + 520)
            i0 = ulo - (512 * b - 8)
            i1 = uhi - (512 * b - 8)
            if i0 > 0:
                P.op("pool", lambda e: e.memset(Ub[s][:, :, 0:i0], 0.0), w=[("Ub", s)])
            if i1 < 528:
                P.op("pool", lambda e: e.memset(Ub[s][:, :, i1:528], 0.0), w=[("Ub", s)])
            P.op("sp", lambda e: e.dma_start(out=Ub[s][:, :, i0:i1],
                                             in_=UT_d[:, :, T0 + ulo:T0 + uhi].rearrange("c p t -> p c t")),
                 r=[("dram", "u", n) for n in range(NBLK)], w=[("Ub", s)], dma=f"Ub_{s}")

        xcnt = [0]

        def pooling(idx):
            sq_, b = blocks[idx]
            s = idx % 2
            U = Ub[s]
            eng = "pool"
            P.op(eng, lambda e: e.tensor_tensor(out=T1[:, :, 1:528], in0=U[:, :, 0:527], in1=U[:, :, 1:528], op=ALU.add),
                 r=[("Ub", s)], w=["T1"])
            P.op(eng, lambda e: e.tensor_tensor(out=T2[:, 0:3, 2:527], in0=T1[:, 1:4, 1:526], in1=T1[:, 1:4, 3:528], op=ALU.add),
                 r=["T1"], w=["T2"])
            P.op(eng, lambda e: e.tensor_tensor(out=T1[:, 2:4, 4:525], in0=T2[:, 1:3, 2:523], in1=T2[:, 1:3, 6:527], op=ALU.add),
                 r=["T2"], w=["T1"])
            P.op(eng, lambda e: e.tensor_tensor(out=T2[:, 2, 8:521], in0=T1[:, 3, 4:517], in1=T1[:, 3, 12:525], op=ALU.add),
                 r=["T1"], w=["T2"])
            srcs = [T1[:, 0, :], T2[:, 0, :], T1[:, 2, :], T2[:, 2, :]]
            for g in range(4):
                w_ = 2 << g
                P.op("dve", lambda e, g=g, w_=w_: e.scalar_tensor_tensor(
                    out=dT[:, g, :], in0=srcs[g][:, 8:520], scalar=1.0 / w_, in1=U[:, g, 8:520],
                    op0=ALU.mult, op1=ALU.subtract),
                    r=["T1", "T2", ("Ub", s)], w=[("dT", g)])
            if b == 0 or b == 7:
                c0 = 0 if b == 0 else 504
                inv = invF if b == 0 else invL
                for g in range(4):
                    P.op("dve", lambda e, g=g: e.tensor_tensor(out=tmp8[:, g, :], in0=srcs[g][:, 8 + c0:16 + c0],
                                                               in1=inv[:, g, :], op=ALU.mult),
                         r=["T1", "T2", "invF", "invL"], w=[("tmp8", g)])
                    P.op("dve", lambda e, g=g: e.tensor_tensor(out=dT[:, g, c0:c0 + 8], in0=tmp8[:, g, :],
                                                               in1=U[:, g, 8 + c0:16 + c0], op=ALU.subtract),
                         r=[("tmp8", g), ("Ub", s)], w=[("dT", g)])
        def pooling_mm(idx):
            for g in range(4):
                bk = next_gbank()
                pb = bank(bk)
                P.op("pe", lambda e, g=g, pb=pb: e.matmul(pb, wpool[:, g, :], dT[:, g, :], start=True, stop=True),
                     r=["wpool", ("dT", g)], w=[("bank", bk)])
                P.op("dve", lambda e, g=g, pb=pb: e.tensor_scalar(mixT[:, 4 + g, :], pb, psc[:, g:g + 1], None, ALU.mult),
                     r=[("bank", bk), "psc"], w=[("mixT", 4 + g)])

        pend_tr = [None]
        pend_atr = []

        def flush_atr():
            for f in pend_atr:
                f()
            del pend_atr[:]

        def attention(idx):
            for jp in range(2):
                rowpair2(idx, 2 * jp, 2 * jp + 1)
                if jp == 0:
                    pooling_mm(idx)
            flush_atr()

        class RP:
            pass

        def rowpair2(idx, jA, jB):
            sq_, b = blocks[idx]
            s = idx % 2
            ws_tile = 4 * b - 2
            rps = []
            for slot, j in enumerate((jA, jB)):
                rp = RP()
                rp.slot = slot
                rp.j = j
                rp.r = 8 * b + 2 * j
                rp.ks = min(max(rp.r - 4, 0), 56)
                rp.nt = 5 if 4 <= rp.r <= 58 else 4
                var = {0: 1, 2: 2, 60: 3, 62: 4}.get(rp.r, 0)
                rp.tab = tabI if var == 0 else tabBs[var]
                rp.tabname = "tabI" if var == 0 else ("tabB", var)
                rp.W = rp.nt * 128
                rp.qs = slice(j * 128, (j + 1) * 128)
                rp.S = Sps[slot]
                rp.O = Ops[slot]
                rps.append(rp)

            def qk(rp, h):
                hp = h // 2
                Q = QA[s] if h % 2 == 0 else QB[s]
                Qn = "QA" if h % 2 == 0 else "QB"
                for t in range(rp.nt):
                    jt = rp.ks // 2 + t - ws_tile
                    P.op("pe", lambda e, t=t, jt=jt: e.matmul(rp.S[:, t * 128:(t + 1) * 128],
                                                            KTb[s][:, hp, jt * 128:(jt + 1) * 128],
                                                            Q[:, hp, rp.qs], start=True, stop=False),
                         r=[("KTb", s), (Qn, s)], w=[("S", rp.slot)])
                    P.op("pe", lambda e, t=t: e.matmul(rp.S[:, t * 128:(t + 1) * 128], ident[:],
                                                      rp.tab[:, h, t * 128:(t + 1) * 128], start=False, stop=True),
                         r=["ident", rp.tabname], w=[("S", rp.slot)])

            def softmax(rp, h):
                pbuf = Pb[rp.slot * 2 + h % 2]
                P.op("act", lambda e: e.activation(out=pbuf[:, 0:rp.nt, :].rearrange("p t q -> p (t q)"),
                                                   in_=rp.S[:, 0:rp.W], func=AF.Exp),
                     r=[("S", rp.slot)], w=[("Pb", rp.slot, h % 2)])

            def pv(rp, h):
                pbuf = Pb[rp.slot * 2 + h % 2]
                for t in range(rp.nt):
                    jt = rp.ks // 2 + t - ws_tile
                    P.op("pe", lambda e, t=t, jt=jt: e.matmul(rp.O[:, h % 4, :], pbuf[:, t, :], Vb[s][:, jt, h, :],
                                                            start=(t == 0), stop=(t == rp.nt - 1)),
                         r=[("Pb", rp.slot, h % 2), ("Vb", s)], w=[("bank", 4 + rp.slot)])

            def norm_half(rp, hh):
                rcb = rc[rp.slot]
                anb = an[rp.slot]
                P.op("dve", lambda e: e.reciprocal(rcb[:], rp.O[:, :, 64]),
                     r=[("bank", 4 + rp.slot)], w=[("rc", rp.slot)])
                P.op("dve", lambda e: e.tensor_tensor(
                    out=anb[:, hh * 256:(hh + 1) * 256].rearrange("p (h d) -> p h d", h=4),
                    in0=rp.O[:, :, 0:64],
                    in1=rcb[:, :].unsqueeze(2).to_broadcast([128, 4, 64]), op=ALU.mult),
                    r=[("bank", 4 + rp.slot), ("rc", rp.slot)], w=[("an", rp.slot, hh)])

            def make_atr(rp):
                def atr():
                    bk = next_gbank()
                    tp = bank_bf(bk)
                    anb = an[rp.slot]
                    for c in range(4):
                        P.op("pe", lambda e, c=c: e.transpose(tp[:, c * 128:(c + 1) * 128], anb[:, c * 128:(c + 1) * 128], ident[:]),
                             r=[("an", rp.slot, 0), ("an", rp.slot, 1), "ident"], w=[("bank", bk)])
                    P.op("dve", lambda e: e.tensor_copy(mixT[:, 0:4, rp.qs], tp[:, 0:512].rearrange("p (c q) -> p c q", c=4)),
                         r=[("bank", bk)], w=[("mixT", 0), ("mixT", 1), ("mixT", 2), ("mixT", 3)])
                return atr

            for h in range(9):
                if h < 8:
                    for rp in rps:
                        qk(rp, h)
                        softmax(rp, h)
                if h == 0:
                    flush_atr()
                if h == 3 and pend_tr[0] is not None:
                    pend_tr[0]()
                    pend_tr[0] = None
                if h >= 1:
                    for rp in rps:
                        pv(rp, h - 1)
                        if h - 1 == 3:
                            norm_half(rp, 0)
                        if h - 1 == 7:
                            norm_half(rp, 1)
            for rp in rps:
                pend_atr.append(make_atr(rp))

        def load_xt(idx, t):
            sq_, b = blocks[idx]
            k = xcnt[0] % 3
            xcnt[0] += 1
            tok = sq_ * SEQ + 512 * b + 128 * t
            P.op("sp", lambda e: e.dma_start(out=xt[k][:], in_=x_d[tok:tok + 128, :]), w=[("xt", k)], dma=f"xt_{k}")
            return k

        def wout_tile(idx, t, k):
            sq_, b = blocks[idx]
            s = idx % 2
            tok = sq_ * SEQ + 512 * b + 128 * t
            for hf in range(2):
                bk = next_gbank()
                pb = bank(bk)
                for kc in range(8):
                    P.op("pe", lambda e, kc=kc, pb=pb, hf=hf: e.matmul(pb, mixT[:, kc, t * 128:(t + 1) * 128],
                                                             wout[:, kc, hf * 512:(hf + 1) * 512],
                                                             start=(kc == 0), stop=(kc == 7)),
                         r=[("mixT", kc), "wout"], w=[("bank", bk)])
                P.op("dve", lambda e, pb=pb, hf=hf: e.tensor_tensor(out=xt[k][:, hf * 512:(hf + 1) * 512], in0=pb,
                                                                   in1=xt[k][:, hf * 512:(hf + 1) * 512], op=ALU.add),
                     r=[("bank", bk)], w=[("xt", k)])
            P.op("sp", lambda e: e.dma_start(out=X2_d[tok:tok + 128, :], in_=xt[k][:]),
                 r=[("xt", k)], w=[("dram", "x2", tok)], dma=f"xt_{k}")
            j = t % 2
            P.op("act", lambda e: e.activation(out=sqx[:], in_=xt[k][:], func=AF.Square), r=[("xt", k)], w=["sqx2"])
            P.op("dve", lambda e: e.tensor_reduce(out=ss2[j][:], in_=sqx[:], axis=AX.X, op=ALU.add),
                 r=["sqx2"], w=[("ss2", j)])
            P.op("act", lambda e: e.activation(out=rt2[j][:], in_=ss2[j][:], func=AF.Ln, bias=EPS, scale=1.0 / D),
                 r=[("ss2", j)], w=[("rt2", j)])
            P.op("act", lambda e: e.activation(out=rstd2[j][:], in_=rt2[j][:], func=AF.Exp, scale=-0.5),
                 r=[("rt2", j)], w=[("rstd2", j)])
            P.op("act", lambda e: e.activation(out=h2b[j][:], in_=xt[k][:], func=AF.Copy, scale=rstd2[j][:, 0:1]),
                 r=[("xt", k), ("rstd2", j)], w=[("h2b", j)])
            def deferred():
                bk = next_gbank()
                tp = bank_bf(bk)
                for c in range(8):
                    P.op("pe", lambda e, c=c: e.transpose(tp[:, c * 128:(c + 1) * 128], h2b[j][:, c * 128:(c + 1) * 128], ident[:]),
                         r=[("h2b", j), "ident"], w=[("bank", bk)])
                P.op("dve", lambda e: e.tensor_tensor(
                    out=h2T[s][:, :, t * 128:(t + 1) * 128], in0=tp.rearrange("p (c q) -> p c q", c=8),
                    in1=g2T[:, :].unsqueeze(2).to_broadcast([128, 8, 128]), op=ALU.mult),
                    r=[("bank", bk), "g2T"], w=[("h2T", s, t)])
                if t == 3:
                    n = idx
                    P.op("sp", lambda e: e.dma_start(out=H2T_d[:, :, n * 512:(n + 1) * 512].rearrange("c p t -> p c t"),
                                                     in_=h2T[s][:]),
                         r=[("h2T", s, tt) for tt in range(4)], w=[("dram", "h2T", n)], dma=f"h2T_{s}")
            return deferred

        loads(0)
        for idx in range(len(blocks)):
            if idx + 1 < len(blocks):
                loads(idx + 1)
            ks_ = [load_xt(idx, 0), load_xt(idx, 1)]
            if "pool" not in SKIP:
                pooling(idx)
            if "attn" not in SKIP:
                attention(idx)
            wide[0] = True
            for t in range(4):
                if t + 2 < 4:
                    ks_.append(load_xt(idx, t + 2))
                d_ = wout_tile(idx, t, ks_[t])
                if pend_tr[0] is not None:
                    pend_tr[0]()
                pend_tr[0] = d_
            wide[0] = False
        if pend_tr[0] is not None:
            pend_tr[0]()

    def phase3():
        wg = A.alloc("wg", [128, 8, DFF], BF16)
        wu = A.alloc("wu", [128, 8, DFF], BF16)
        wd = A.alloc("wd", [128, NFC, 1024], BF16)
        h2T = [A.alloc("h2Tb", [128, 8, 512], BF16) for _ in range(2)]
        x2t = [A.alloc("x2t", [128, 1024], F32) for _ in range(2)]
        actT = [A.alloc("actT", [128, NFC, 512], BF16) for _ in range(2)]
        sg = [A.alloc("sg", [128, 512], F32) for _ in range(2)]

        wg_v = w_gate_d.rearrange("(kc p) n -> p kc n", p=128)
        wu_v = w_up_d.rearrange("(kc p) n -> p kc n", p=128)
        wd_v = w_down_d.rearrange("(f p) n -> p f n", p=128)
        GB = [0, 128, 256, 384, 512, 704, 1408, 2112, DFF]
        for g in range(len(GB) - 1):
            c0, c1 = GB[g], GB[g + 1]
            P.op("pool", lambda e, c0=c0, c1=c1: e.dma_start(out=wg[:, :, c0:c1], in_=wg_v[:, :, c0:c1]),
                 w=[("wg", g)], dma=f"wg{g}")
            P.op("pool", lambda e, c0=c0, c1=c1: e.dma_start(out=wu[:, :, c0:c1], in_=wu_v[:, :, c0:c1]),
                 w=[("wu", g)], dma=f"wu{g}")
        for g in range(2):
            P.op("pool", lambda e, g=g: e.dma_start(out=wd[:, g * 11:(g + 1) * 11, :], in_=wd_v[:, g * 11:(g + 1) * 11, :]),
                 w=[("wd", g)], dma=f"wd{g}")

        def wgrp(f):
            lo, hi = f * 128, f * 128 + 128
            return [g for g in range(len(GB) - 1) if GB[g] < hi and GB[g + 1] > lo]

        def load_h(n):
            s = n % 2
            P.op("sp", lambda e: e.dma_start(out=h2T[s][:], in_=H2T_d[:, :, n * 512:(n + 1) * 512].rearrange("c p t -> p c t")),
                 r=[("dram", "h2T", n)], w=[("h2Tb", s)], dma=f"h2Tb_{s}")

        xc = [0]

        def load_x2(n, t):
            k = xc[0] % 2
            xc[0] += 1
            tok = n * 512 + t * 128
            P.op("sp", lambda e: e.dma_start(out=x2t[k][:], in_=X2_d[tok:tok + 128, :]),
                 r=[("dram", "x2", tok)], w=[("x2t", k)], dma=f"x2t_{k}")
            return k

        gu = [0]
        dn = [0]

        def gate_up(n, f, a):
            s = n % 2
            i3 = gu[0] % 3
            gu[0] += 1
            bg = 2 * i3
            bu = 2 * i3 + 1
            pg = bank(bg)
            pu = bank(bu)
            for kc in range(8):
                P.op("pe", lambda e, kc=kc: e.matmul(pg, wg[:, kc, f * 128:(f + 1) * 128], h2T[s][:, kc, :],
                                                    start=(kc == 0), stop=(kc == 7)),
                     r=[("wg", g) for g in wgrp(f)] + [("h2Tb", s)], w=[("bank", bg)])
            for kc in range(8):
                P.op("pe", lambda e, kc=kc: e.matmul(pu, wu[:, kc, f * 128:(f + 1) * 128], h2T[s][:, kc, :],
                                                    start=(kc == 0), stop=(kc == 7)),
                     r=[("wu", g) for g in wgrp(f)] + [("h2Tb", s)], w=[("bank", bu)])
            j = gu[0] % 2
            P.op("act", lambda e: e.activation(out=sg[j][:], in_=pg, func=AF.Silu),
                 r=[("bank", bg)], w=[("sg", j)])
            P.op("dve", lambda e: e.tensor_tensor(out=actT[a][:, f, :], in0=pu, in1=sg[j][:], op=ALU.mult),
                 r=[("bank", bu), ("sg", j)], w=[("actT", a, f)])

        def down(n, a):
            ks_ = {0: load_x2(n, 0), 1: load_x2(n, 1)}
            for t in range(4):
                k = ks_[t]
                for hf in range(2):
                    bk = 6 + (dn[0] % 2)
                    dn[0] += 1
                    pb = bank(bk)
                    for f in range(NFC):
                        P.op("pe", lambda e, f=f, pb=pb, hf=hf, t=t: e.matmul(pb, actT[a][:, f, t * 128:(t + 1) * 128],
                                                                             wd[:, f, hf * 512:(hf + 1) * 512],
                                                                             start=(f == 0), stop=(f == NFC - 1)),
                             r=[("actT", a, f), ("wd", f // 11)], w=[("bank", bk)])
                    P.op("dve", lambda e, pb=pb, hf=hf, k=k: e.tensor_tensor(out=x2t[k][:, hf * 512:(hf + 1) * 512], in0=pb,
                                                                             in1=x2t[k][:, hf * 512:(hf + 1) * 512], op=ALU.add),
                         r=[("bank", bk)], w=[("x2t", k)])
                tok = n * 512 + t * 128
                P.op("sp", lambda e, k=k, tok=tok: e.dma_start(out=out_d[tok:tok + 128, :], in_=x2t[k][:]),
                     r=[("x2t", k)], w=[("dram", "out", tok)], dma=f"x2t_{k}")
                if t + 2 < 4:
                    ks_[t + 2] = load_x2(n, t + 2)

        load_h(0)
        load_h(1)
        for f in range(NFC):
            gate_up(0, f, 0)
            gate_up(1, f, 1)
        load_h(2)
        down(0, 0)
        down(1, 1)
        for n in range(2, NBLK):
            if n + 1 < NBLK:
                load_h(n + 1)
            for f in range(NFC):
                gate_up(n, f, n % 2)
            down(n, n % 2)

    if 1 in phases:
        phase1()
    P.barrier()
    A.reset(phase_mark)
    if 2 in phases:
        phase2()
    P.barrier()
    A.reset(phase_mark)
    A.top = top_save
    if 3 in phases:
        phase3()
    P.final_wait("sp")
    P.emit(nc)
    return nc


def _tab_index():
    idx = np.zeros((5, 128, 640), dtype=np.int64)
    PAD = 15 * 31
    p = np.arange(128)
    qi = np.arange(128)
    for v, r in enumerate([8, 0, 2, 60, 62]):
        ks = min(max(r - 4, 0), 56)
        nt = 5 if 4 <= r <= 58 else 4
        for t in range(5):
            kr = ks + 2 * t + p[:, None] // 64
            kc = p[:, None] % 64
            rq = r + qi[None, :] // 64
            qc = qi[None, :] % 64
            rs = np.clip(rq - 4, 0, 56)
            cs = np.clip(qc - 8, 0, 48)
            valid = (kr >= rs) & (kr < rs + 8) & (kc >= cs) & (kc < cs + 16) & (t < nt)
            ii = (kr - rq + 7) * 31 + (kc - qc + 15)
            idx[v, :, t * 128:(t + 1) * 128] = np.where(valid, ii, PAD)
    return idx


def _inv_tables():
    invF = np.zeros((4, 8), np.float32)
    invL = np.zeros((4, 8), np.float32)
    for g, w in enumerate((2, 4, 8, 16)):
        for j in range(8):
            t = j
            cnt = min(t + w // 2, SEQ) - max(t - w // 2, 0)
            invF[g, j] = 1.0 / cnt
            t = SEQ - 8 + j
            cnt = min(t + w // 2, SEQ) - max(t - w // 2, 0)
            invL[g, j] = 1.0 / cnt
    return (np.ascontiguousarray(np.broadcast_to(invF[None], (128, 4, 8))),
            np.ascontiguousarray(np.broadcast_to(invL[None], (128, 4, 8))))


def make_in_maps(x, norm1_g, w_in, q_norm_g, k_norm_g, rpb, w_pool, pool_scale, w_out,
                 norm2_g, w_gate, w_up, w_down):
    f = lambda a: np.ascontiguousarray(np.asarray(a, dtype=np.float32))
    x = f(x).reshape(N_CORES, TOK, D)
    rpb_pad = np.concatenate([f(rpb)[0].reshape(8, 15 * 31), np.full((8, 1), NEG, np.float32)], axis=1)
    idx = _tab_index()
    tab = np.ascontiguousarray(rpb_pad[:, idx].transpose(1, 2, 0, 3))
    invF, invL = _inv_tables()
    bdm = np.zeros((128, 128), np.float32)
    bdm[:64, :64] = 1.0
    bdm[64:, 64:] = 1.0
    common = dict(
        w_in=f(w_in)[0], w_out=f(w_out)[0], w_gate=f(w_gate)[0], w_up=f(w_up)[0], w_down=f(w_down)[0],
        w_pool=f(w_pool)[0],
        g1T=np.ascontiguousarray(f(norm1_g)[0].reshape(8, 128).T),
        g2T=np.ascontiguousarray(f(norm2_g)[0].reshape(8, 128).T),
        gq=np.ascontiguousarray(np.tile(f(q_norm_g)[0], 2).reshape(128, 1)),
        gk=np.ascontiguousarray(np.tile(f(k_norm_g)[0], 2).reshape(128, 1)),
        pscale=np.ascontiguousarray(f(pool_scale)[0].reshape(4, 128).T),
        tab=tab, ident=np.eye(128, dtype=np.float32), bd=bdm, invF=invF, invL=invL,
    )
    return [dict(common, x=np.ascontiguousarray(x[c])) for c in range(N_CORES)]


_NC_CACHE = {}


def kernel(x, norm1_g, w_in, q_norm_g, k_norm_g, rpb, w_pool, pool_scale, w_out,
           norm2_g, w_gate, w_up, w_down):
    in_maps = make_in_maps(x, norm1_g, w_in, q_norm_g, k_norm_g, rpb, w_pool, pool_scale, w_out,
                           norm2_g, w_gate, w_up, w_down)
    if "nc" not in _NC_CACHE:
        _NC_CACHE["nc"] = build_nc()
    nc = _NC_CACHE["nc"]
    res = run_bass_kernel_spmd(nc, in_maps, core_ids=list(range(N_CORES)))
    out = np.stack([np.asarray(r["out"], dtype=np.float32) for r in res.results], axis=0)
    return out.reshape(16, SEQ, D)
```

```python
import numpy as np
import concourse.bass as bass
import concourse.mybir as mybir
from concourse.bass_utils import run_bass_kernel_spmd

F32 = mybir.dt.float32
BF16 = mybir.dt.bfloat16
ALU = mybir.AluOpType
AF = mybir.ActivationFunctionType
AX = mybir.AxisListType

N_CORES = 8
TOK = 8192
SEQ = 4096
D = 1024
NBLK = 16
DFF = 2816
NFC = 22
EPS = 1e-6
NEG = -30000.0
SEM_ROT = 30000
import os
SKIP = set(os.environ.get('KSKIP', '').split(','))


class Prog:
    ENG = ("sp", "act", "pe", "dve", "pool")

    def __init__(self):
        self.ops = []
        self.tiles = {}
        self.dcount = {}
        self.last = {e: None for e in self.ENG}
        self.bar = {e: None for e in self.ENG}

    def op(self, eng, fn, r=(), w=(), dma=None):
        i = len(self.ops)
        deps = set()
        ddeps = {}

        def add(j):
            o = self.ops[j]
            if o["dma"] is not None:
                c = o["dma"]
                ddeps[c] = max(ddeps.get(c, 0), self.dcount[c])
            else:
                deps.add(j)

        for t in r:
            st = self.tiles.setdefault(t, [None, []])
            if st[0] is not None:
                add(st[0])
        for t in w:
            st = self.tiles.setdefault(t, [None, []])
            if st[0] is not None:
                add(st[0])
            for j in st[1]:
                add(j)
        for t in r:
            self.tiles[t][1].append(i)
        for t in w:
            st = self.tiles[t]
            st[0] = i
            st[1] = []
        if self.bar[eng] is not None:
            bd, bdd = self.bar[eng]
            deps |= bd
            for c, v in bdd.items():
                ddeps[c] = max(ddeps.get(c, 0), v)
            self.bar[eng] = None
        latest = {}
        for j in deps:
            en = self.ops[j]["eng"]
            if en not in latest or j > latest[en]:
                latest[en] = j
        deps = set(latest.values())
        if dma is not None:
            self.dcount[dma] = self.dcount.get(dma, 0) + 1
        self.ops.append(dict(eng=eng, fn=fn, deps=deps, ddeps=ddeps, dma=dma))
        self.last[eng] = i
        return i

    def barrier(self):
        bd = set(j for j in self.last.values() if j is not None and self.ops[j]["dma"] is None)
        for e in self.ENG:
            for j in range(len(self.ops) - 1, -1, -1):
                if self.ops[j]["eng"] == e and self.ops[j]["dma"] is None and self.ops[j]["fn"] is not None:
                    bd.add(j)
                    break
        bdd = dict(self.dcount)
        for e in self.ENG:
            self.bar[e] = (set(bd), dict(bdd))

    def final_wait(self, eng="sp"):
        self.bar[eng] = (set(), dict(self.dcount))
        self.op(eng, None)

    def emit(self, nc):
        ops = self.ops
        need = [False] * len(ops)
        for o in ops:
            for j in o["deps"]:
                if not (o["eng"] == "pe" and ops[j]["eng"] == "pe"):
                    need[j] = True
        cnt = {e: 0 for e in self.ENG}
        sig = [None] * len(ops)
        for i, o in enumerate(ops):
            if need[i]:
                c = cnt[o["eng"]]
                sig[i] = (o["eng"], c // SEM_ROT, c % SEM_ROT + 1)
                cnt[o["eng"]] = c + 1
        esem = {}
        for e in self.ENG:
            for k in range(cnt[e] // SEM_ROT + 1):
                esem[(e, k)] = nc.alloc_semaphore(f"s_{e}_{k}")
        dsem = {c: nc.alloc_semaphore(f"d_{c}") for c in self.dcount}
        for c, v in self.dcount.items():
            assert v * 16 < 60000, (c, v)
        self.n_sems = len(esem) + len(dsem)

        def run(engname):
            def body(e):
                waited = {}
                for i, o in enumerate(ops):
                    if o["eng"] != engname:
                        continue
                    waits = {}
                    for j in o["deps"]:
                        if engname == "pe" and ops[j]["eng"] == "pe":
                            continue
                        en, k, v = sig[j]
                        key = ("e", en, k)
                        waits[key] = max(waits.get(key, 0), v)
                    for c, v in o["ddeps"].items():
                        key = ("d", c)
                        waits[key] = max(waits.get(key, 0), 16 * v)
                    for key, v in waits.items():
                        if waited.get(key, 0) >= v:
                            continue
                        if key[0] == "e":
                            later = [kk for kk in waited if kk[0] == "e" and kk[1] == key[1] and kk[2] > key[2]]
                            if later:
                                continue
                            e.wait_ge(esem[(key[1], key[2])], v)
                        else:
                            e.wait_ge(dsem[key[1]], v)
                        waited[key] = v
                    if o["fn"] is None:
                        continue
                    ins = o["fn"](e)
                    if o["dma"] is not None:
                        ins.then_inc(dsem[o["dma"]], 16)
                    elif sig[i] is not None:
                        ins.then_inc(esem[(sig[i][0], sig[i][1])], 1)
            return body

        with nc.Block() as block:
            block.sync(run("sp"))
            block.scalar(run("act"))
            block.tensor(run("pe"))
            block.vector(run("dve"))
            block.gpsimd(run("pool"))


class Arena:
    def __init__(self, nc):
        self.nc = nc
        self.base = (nc.sbuf_base + 63) // 64 * 64
        self.top = nc.sbuf_top
        self.cur = self.base
        self.n = 0

    def alloc(self, name, shape, dt):
        sz = int(np.prod(shape[1:])) * (2 if dt == BF16 else 4)
        sz = (sz + 63) // 64 * 64
        off = self.cur
        assert off + sz <= self.top, (name, off, sz, self.top)
        self.cur += sz
        self.n += 1
        return self.nc.alloc_sbuf_tensor_at(f"{name}_{self.n}", list(shape), dt, offset=off)

    def alloc_top(self, name, shape, dt):
        sz = int(np.prod(shape[1:])) * (2 if dt == BF16 else 4)
        sz = (sz + 63) // 64 * 64
        self.top -= sz
        self.top = self.top // 64 * 64
        assert self.top >= self.cur, (name, self.top, self.cur)
        self.n += 1
        return self.nc.alloc_sbuf_tensor_at(f"{name}_{self.n}", list(shape), dt, offset=self.top)

    def mark(self):
        return self.cur

    def reset(self, m):
        self.cur = m


def build_nc(debug=False, phases=(1, 2, 3)):
    nc = bass.Bass("TRN2", target_bir_lowering=False)
    P = Prog()
    A = Arena(nc)

    def din(name, shape, dt=F32):
        return nc.dram_tensor(name, list(shape), dt, kind="ExternalInput").ap()

    x_d = din("x", [TOK, D])
    w_in_d = din("w_in", [D, 2048])
    w_out_d = din("w_out", [D, D])
    w_gate_d = din("w_gate", [D, DFF])
    w_up_d = din("w_up", [D, DFF])
    w_down_d = din("w_down", [DFF, D])
    w_pool_d = din("w_pool", [4, 128, 128])
    g1T_d = din("g1T", [128, 8])
    g2T_d = din("g2T", [128, 8])
    gq_d = din("gq", [128, 1])
    gk_d = din("gk", [128, 1])
    psc_d = din("pscale", [128, 4])
    tab_d = din("tab", [5, 128, 8, 640])
    ident_d = din("ident", [128, 128])
    bd_d = din("bd", [128, 128])
    invF_d = din("invF", [128, 4, 8])
    invL_d = din("invL", [128, 4, 8])
    out_d = nc.dram_tensor("out", [TOK, D], F32, kind="ExternalOutput").ap()

    skind = "ExternalOutput" if debug else "Internal"
    QT_d = nc.dram_tensor("QT", [4, 128, TOK], BF16, kind=skind).ap()
    KT_d = nc.dram_tensor("KT", [4, 128, TOK], BF16, kind=skind).ap()
    V_d = nc.dram_tensor("V", [TOK, 520], BF16, kind=skind).ap()
    UT_d = nc.dram_tensor("UT", [4, 128, TOK], F32, kind=skind).ap()
    X2_d = nc.dram_tensor("X2", [TOK, D], F32, kind=skind).ap()
    H2T_d = nc.dram_tensor("H2T", [8, 128, TOK], BF16, kind=skind).ap()

    PS = nc.alloc_psum_tensor("ps", [128, 4096], F32)

    def bank(k, n=1):
        return PS[:, 512 * k:512 * (k + n)]

    def bank_bf(k):
        return PS[:, 512 * k:512 * (k + 1)].bitcast(BF16)

    ident = A.alloc("ident", [128, 128], BF16)
    bd = A.alloc("bd", [128, 128], BF16)
    g1T = A.alloc("g1T", [128, 8], F32)
    g2T = A.alloc("g2T", [128, 8], F32)
    gq = A.alloc("gq", [128, 1], F32)
    gk = A.alloc("gk", [128, 1], F32)
    gk8 = A.alloc("gk8", [128, 1], F32)
    psc = A.alloc("psc", [128, 4], F32)
    P.op("pool", lambda e: e.dma_start(out=ident[:], in_=ident_d), w=["ident"], dma="c_ident")
    P.op("pool", lambda e: e.dma_start(out=bd[:], in_=bd_d), w=["bd"], dma="c_bd")
    P.op("sp", lambda e: e.dma_start(out=g1T[:], in_=g1T_d), w=["g1T"], dma="c_g1")
    P.op("sp", lambda e: e.dma_start(out=g2T[:], in_=g2T_d), w=["g2T"], dma="c_g2")
    P.op("sp", lambda e: e.dma_start(out=gq[:], in_=gq_d), w=["gq"], dma="c_gq")
    P.op("sp", lambda e: e.dma_start(out=gk[:], in_=gk_d), w=["gk"], dma="c_gk")
    P.op("sp", lambda e: e.dma_start(out=psc[:], in_=psc_d), w=["psc"], dma="c_psc")
    P.op("dve", lambda e: e.tensor_scalar(gk8[:], gk[:], 8.0, None, ALU.mult), r=["gk"], w=["gk8"])
    phase_mark = A.mark()
    top_save = A.top
    tabI = A.alloc_top("tabI", [128, 8, 640], BF16)
    tabBs = [None] + [A.alloc_top("tabB", [128, 8, 512], BF16) for _ in range(4)]
    wpool = A.alloc_top("wpool", [128, 4, 128], BF16)
    top_p12 = A.top

    def early_phase2_loads():
        for v in (1, 2):
            P.op("pool", lambda e, v=v: e.dma_start(out=tabBs[v][:], in_=tab_d[v][:, :, 0:512]),
                 w=[("tabB", v)], dma=f"tabB{v}")
        P.op("pool", lambda e: e.dma_start(out=tabI[:], in_=tab_d[0]), w=["tabI"], dma="tabI")
        P.op("pool", lambda e: e.dma_start(out=wpool[:], in_=w_pool_d.rearrange("g c d -> c g d")),
             w=["wpool"], dma="wpool")
        for v in (3, 4):
            P.op("pool", lambda e, v=v: e.dma_start(out=tabBs[v][:], in_=tab_d[v][:, :, 0:512]),
                 w=[("tabB", v)], dma=f"tabB{v}")

    def phase1():
        win = A.alloc("win", [128, 8, 2048], BF16)
        x1 = [A.alloc("x1", [128, 4, 1024], F32) for _ in range(2)]
        sqx = A.alloc("sqx", [128, 4, 1024], F32)
        ss = A.alloc("ss", [128, 4], F32)
        rt = A.alloc("rt", [128, 4], F32)
        rstd = A.alloc("rstd", [128, 4], F32)
        hb = [A.alloc("hb", [128, 1024], BF16) for _ in range(4)]
        hT = [A.alloc("hT", [128, 8, 512], BF16) for _ in range(2)]
        sq = [A.alloc("sq", [128, 512], BF16) for _ in range(2)]
        rs = [A.alloc("rs", [128, 512], F32) for _ in range(2)]
        rr = [A.alloc("rr", [128, 512], F32) for _ in range(2)]
        qst = [A.alloc("qst", [128, 4, 512], BF16) for _ in range(2)]
        kst = [A.alloc("kst", [128, 4, 512], BF16) for _ in range(2)]
        vst = [A.alloc("vst", [128, 4, 8, 65], BF16) for _ in range(2)]
        ust = [A.alloc("ust", [128, 4, 512], F32) for _ in range(2)]

        w_in_v = w_in_d.rearrange("(kc p) n -> p kc n", p=128)
        for g in range(4):
            P.op("pool", lambda e, g=g: e.dma_start(out=win[:, :, g * 512:(g + 1) * 512],
                                                    in_=w_in_v[:, :, g * 512:(g + 1) * 512]),
                 w=[("win", g)], dma=f"win{g}")
        for s in range(2):
            P.op("pool", lambda e, s=s: e.memset(vst[s][:], 1.0), w=[("vst", s)])
        early_phase2_loads()

        def load_x(n):
            s = n % 2
            P.op("sp", lambda e: e.dma_start(
                out=x1[s][:], in_=x_d[n * 512:(n + 1) * 512, :].rearrange("(t p) d -> p t d", p=128)),
                w=[("x1", s, t) for t in range(4)], dma=f"x1_{s}")

        def norm_step(n, k):
            s = n % 2
            if k < 4:
                t = k
                P.op("act", lambda e: e.activation(out=sqx[:, t, :], in_=x1[s][:, t, :], func=AF.Square),
                     r=[("x1", s, t)], w=[("sqx", t)])
            elif k < 8:
                t = k - 4
                P.op("dve", lambda e: e.tensor_reduce(out=ss[:, t:t + 1], in_=sqx[:, t, :], axis=AX.X, op=ALU.add),
                     r=[("sqx", t)], w=[("ss", t)])
            elif k == 8:
                P.op("act", lambda e: e.activation(out=rt[:], in_=ss[:], func=AF.Ln, bias=EPS, scale=1.0 / D),
                     r=[("ss", t) for t in range(4)], w=["rt"])
            elif k == 9:
                P.op("act", lambda e: e.activation(out=rstd[:], in_=rt[:], func=AF.Exp, scale=-0.5), r=["rt"], w=["rstd"])
            elif k < 14:
                t = k - 10
                P.op("act", lambda e: e.activation(out=hb[t][:], in_=x1[s][:, t, :], func=AF.Copy,
                                                   scale=rstd[:, t:t + 1]),
                     r=[("x1", s, t), "rstd"], w=[("hb", t)])
            else:
                t = k - 14
                j = t % 2
                tp = bank_bf(j)
                for c in range(8):
                    P.op("pe", lambda e, c=c: e.transpose(tp[:, c * 128:(c + 1) * 128], hb[t][:, c * 128:(c + 1) * 128], ident[:]),
                         r=[("hb", t), "ident"], w=[("bank", j)])
                P.op("dve", lambda e: e.tensor_tensor(
                    out=hT[s][:, :, t * 128:(t + 1) * 128],
                    in0=tp.rearrange("p (c q) -> p c q", c=8),
                    in1=g1T[:, :].unsqueeze(2).to_broadcast([128, 8, 128]), op=ALU.mult),
                    r=[("bank", j), "g1T"], w=[("hT", s, t)])
        NSTEP = 18
        SCHED = {0: [0, 1], 1: [2, 4], 2: [3, 5], 3: [6], 4: [7], 5: [8], 6: [9], 7: [10], 8: [11], 9: [12, 14],
                 10: [13], 11: [15], 12: [16], 13: [17], 14: [], 15: []}

        mmb = [0]

        def next_bank():
            b = 2 + (mmb[0] % 5)
            mmb[0] += 1
            return b

        smb = [0]

        def qk_A(n, c):
            s = n % 2
            b = next_bank()
            pb = bank(b)
            for kc in range(8):
                P.op("pe", lambda e, kc=kc: e.matmul(pb, win[:, kc, c * 128:(c + 1) * 128], hT[s][:, kc, :],
                                                    start=(kc == 0), stop=(kc == 7)),
                     r=[("win", c // 4)] + [("hT", s, t) for t in range(4)], w=[("bank", b)])
            j = smb[0] % 2
            smb[0] += 1
            P.op("act", lambda e: e.activation(out=sq[j][:], in_=pb, func=AF.Square),
                 r=[("bank", b)], w=[("sq", j)])
            return (n, c, b, j)

        def qk_B(st_):
            n, c, b, j = st_
            s = n % 2
            pb = bank(b)
            isq = c < 4
            st = qst[s] if isq else kst[s]
            stname = "qst" if isq else "kst"
            b2 = 7
            pb2 = bank(b2)
            P.op("pe", lambda e: e.matmul(pb2, bd[:], sq[j][:], start=True, stop=True),
                 r=["bd", ("sq", j)], w=[("bank", b2)])
            P.op("act", lambda e: e.activation(out=rs[j][:], in_=pb2, func=AF.Ln, bias=64.0 * EPS, scale=1.0),
                 r=[("bank", b2)], w=[("rs", j)])
            P.op("act", lambda e: e.activation(out=rr[j][:], in_=rs[j][:], func=AF.Exp, scale=-0.5),
                 r=[("rs", j)], w=[("rr", j)])
            gv = gq if isq else gk8
            P.op("dve", lambda e: e.scalar_tensor_tensor(out=st[:, c % 4, :], in0=pb, scalar=gv[:, 0:1], in1=rr[j][:],
                                                         op0=ALU.mult, op1=ALU.mult),
                 r=[("bank", b), ("rr", j), "gq", "gk8"], w=[(stname, s, c % 4)])
            if c % 4 == 3:
                dst = QT_d if isq else KT_d
                P.op("sp", lambda e: e.dma_start(out=dst[:, :, n * 512:(n + 1) * 512].rearrange("c p t -> p c t"),
                                                 in_=st[:]),
                     r=[(stname, s, cc) for cc in range(4)], w=[("dram", stname, n)], dma=f"{stname}_{s}")

        def v_tile(n, t):
            s = n % 2
            b = next_bank()
            pb = bank(b)
            for kc in range(8):
                P.op("pe", lambda e, kc=kc: e.matmul(pb, hT[s][:, kc, t * 128:(t + 1) * 128], win[:, kc, 1024:1536],
                                                    start=(kc == 0), stop=(kc == 7)),
                     r=[("win", 2), ("hT", s, t)], w=[("bank", b)])
            P.op("dve", lambda e: e.tensor_copy(vst[s][:, t, :, 0:64], pb.rearrange("p (h d) -> p h d", h=8)),
                 r=[("bank", b)], w=[("vst", s)])
            if t == 3:
                P.op("sp", lambda e: e.dma_start(
                    out=V_d[n * 512:(n + 1) * 512, :].rearrange("(t p) e -> p t e", p=128),
                    in_=vst[s][:].rearrange("p t h d -> p t (h d)")),
                    r=[("vst", s)], w=[("dram", "v", n)], dma=f"vst_{s}")

        def u_chunk(n, g):
            s = n % 2
            b = next_bank()
            pb = bank(b)
            c = 12 + g
            for kc in range(8):
                P.op("pe", lambda e, kc=kc: e.matmul(pb, win[:, kc, c * 128:(c + 1) * 128], hT[s][:, kc, :],
                                                    start=(kc == 0), stop=(kc == 7)),
                     r=[("win", 3)] + [("hT", s, t) for t in range(4)], w=[("bank", b)])
            P.op("dve", lambda e: e.tensor_copy(ust[s][:, g, :], pb),
                 r=[("bank", b)], w=[("ust", s, g)])
            if g == 3:
                P.op("sp", lambda e: e.dma_start(out=UT_d[:, :, n * 512:(n + 1) * 512].rearrange("c p t -> p c t"),
                                                 in_=ust[s][:]),
                     r=[("ust", s, gg) for gg in range(4)], w=[("dram", "u", n)], dma=f"ust_{s}")

        load_x(0)
        load_x(1)
        for k in range(NSTEP):
            norm_step(0, k)
        units = [("qk", 0), ("v", 0), ("qk", 1), ("u", 0), ("qk", 2), ("v", 1), ("qk", 3), ("u", 1),
                 ("qk", 4), ("v", 2), ("qk", 5), ("u", 2), ("qk", 6), ("v", 3), ("qk", 7), ("u", 3)]
        for n in range(NBLK):
            pend = None
            if n + 2 < NBLK:
                load_x(n + 2)
            for i, (kind, a) in enumerate(units):
                if kind == "qk":
                    st_ = qk_A(n, a)
                    if pend is not None:
                        qk_B(pend)
                    pend = st_
                elif kind == "v":
                    v_tile(n, a)
                else:
                    u_chunk(n, a)
                if n + 1 < NBLK:
                    for k in SCHED[i]:
                        norm_step(n + 1, k)
            qk_B(pend)

    def phase2():
        wout = A.alloc("wout", [128, 8, 1024], BF16)
        invF = A.alloc("invF", [128, 4, 8], F32)
        invL = A.alloc("invL", [128, 4, 8], F32)
        KTb = [A.alloc("KTb", [128, 4, 1024], BF16) for _ in range(2)]
        Vb = [A.alloc("Vb", [128, 8, 8, 65], BF16) for _ in range(2)]
        QA = [A.alloc("QA", [128, 4, 512], BF16) for _ in range(2)]
        QB = [A.alloc("QB", [128, 4, 512], BF16) for _ in range(2)]
        Ub = [A.alloc("Ub", [128, 4, 528], F32) for _ in range(2)]
        T1 = A.alloc("T1", [128, 4, 528], F32)
        T2 = A.alloc("T2", [128, 3, 528], F32)
        tmp8 = A.alloc("tmp8", [128, 4, 8], F32)
        dT = A.alloc("dT", [128, 4, 512], BF16)
        xt = [A.alloc("xt", [128, 1024], F32) for _ in range(3)]
        mixT = A.alloc("mixT", [128, 8, 512], BF16)
        Pb = [A.alloc("Pb", [128, 5, 128], BF16) for _ in range(4)]
        rc = [A.alloc("rc", [128, 4], F32) for _ in range(2)]
        an = [A.alloc("an", [128, 512], BF16) for _ in range(2)]
        sqx = A.alloc("sqx2", [128, 1024], F32)
        ss2 = [A.alloc("ss2", [128, 1], F32) for _ in range(2)]
        st6 = [A.alloc("st6", [128, 2, 6], F32) for _ in range(2)]
        mv = [A.alloc("mv", [128, 2], F32) for _ in range(2)]
        rt2 = [A.alloc("rt2", [128, 1], F32) for _ in range(2)]
        rstd2 = [A.alloc("rstd2", [128, 1], F32) for _ in range(2)]
        h2b = [A.alloc("h2b", [128, 1024], BF16) for _ in range(2)]
        h2T = [A.alloc("h2T", [128, 8, 512], BF16) for _ in range(2)]

        Sps = [PS[:, 0:640], PS[:, 1024:1664]]
        Ops = [PS[:, 512 * 4:512 * 4 + 260].rearrange("p (h e) -> p h e", h=4),
               PS[:, 512 * 5:512 * 5 + 260].rearrange("p (h e) -> p h e", h=4)]
        gb = [0]

        wide = [False]

        def next_gbank():
            if wide[0]:
                b = 4 + (gb[0] % 4)
            else:
                b = 6 + (gb[0] % 2)
            gb[0] += 1
            return b

        P.op("pool", lambda e: e.dma_start(out=wout[:], in_=w_out_d.rearrange("(kc p) n -> p kc n", p=128)),
             w=["wout"], dma="wout")
        P.op("sp", lambda e: e.dma_start(out=invF[:], in_=invF_d), w=["invF"], dma="c_invF")
        P.op("sp", lambda e: e.dma_start(out=invL[:], in_=invL_d), w=["invL"], dma="c_invL")
        for s in range(2):
            P.op("pool", lambda e, s=s: e.memset(QA[s][:], 0.0), w=[("QA", s)])
            P.op("pool", lambda e, s=s: e.memset(QB[s][:], 0.0), w=[("QB", s)])

        blocks = [(sq_, b) for sq_ in range(2) for b in range(8)]

        def loads(idx):
            sq_, b = blocks[idx]
            s = idx % 2
            T0 = sq_ * SEQ
            ws = 512 * b - 256
            lo = max(0, ws)
            hi = min(SEQ, ws + 1024)
            P.op("sp", lambda e: e.dma_start(out=KTb[s][:, :, lo - ws:hi - ws],
                                             in_=KT_d[:, :, T0 + lo:T0 + hi].rearrange("c p t -> p c t")),
                 r=[("dram", "kst", n) for n in range(NBLK)], w=[("KTb", s)], dma=f"KTb_{s}")
            j0 = (lo - ws) // 128
            j1 = (hi - ws) // 128
            P.op("sp", lambda e: e.dma_start(
                out=Vb[s][:, j0:j1, :, :].rearrange("p j h e -> p j (h e)"),
                in_=V_d[T0 + lo:T0 + hi, :].rearrange("(j p) e -> p j e", p=128)),
                r=[("dram", "v", n) for n in range(NBLK)], w=[("Vb", s)], dma=f"Vb_{s}")
            q0 = T0 + 512 * b
            P.op("sp", lambda e: e.dma_start(out=QA[s][0:64, :, :],
                                             in_=QT_d[:, 0:64, q0:q0 + 512].rearrange("c p t -> p c t")),
                 r=[("dram", "qst", n) for n in range(NBLK)], w=[("QA", s)], dma=f"QA_{s}")
            P.op("sp", lambda e: e.dma_start(out=QB[s][64:128, :, :],
                                             in_=QT_d[:, 64:128, q0:q0 + 512].rearrange("c p t -> p c t")),
                 r=[("dram", "qst", n) for n in range(NBLK)], w=[("QB", s)], dma=f"QB_{s}")
            ulo = max(0, 512 * b - 8)
            uhi = min(SEQ, 512 * b + 520)
            i0 = ulo - (512 * b - 8)
            i1 = uhi - (512 * b - 8)
            if i0 > 0:
                P.op("pool", lambda e: e.memset(Ub[s][:, :, 0:i0], 0.0), w=[("Ub", s)])
            if i1 < 528:
                P.op("pool", lambda e: e.memset(Ub[s][:, :, i1:528], 0.0), w=[("Ub", s)])
            P.op("sp", lambda e: e.dma_start(out=Ub[s][:, :, i0:i1],
                                             in_=UT_d[:, :, T0 + ulo:T0 + uhi].rearrange("c p t -> p c t")),
                 r=[("dram", "u", n) for n in range(NBLK)], w=[("Ub", s)], dma=f"Ub_{s}")

        xcnt = [0]

        def pooling(idx):
            sq_, b = blocks[idx]
            s = idx % 2
            U = Ub[s]
            eng = "pool"
            P.op(eng, lambda e: e.tensor_tensor(out=T1[:, :, 1:528], in0=U[:, :, 0:527], in1=U[:, :, 1:528], op=ALU.add),
                 r=[("Ub", s)], w=["T1"])
            P.op(eng, lambda e: e.tensor_tensor(out=T2[:, 0:3, 2:527], in0=T1[:, 1:4, 1:526], in1=T1[:, 1:4, 3:528], op=ALU.add),
                 r=["T1"], w=["T2"])
            P.op(eng, lambda e: e.tensor_tensor(out=T1[:, 2:4, 4:525], in0=T2[:, 1:3, 2:523], in1=T2[:, 1:3, 6:527], op=ALU.add),
                 r=["T2"], w=["T1"])
            P.op(eng, lambda e: e.tensor_tensor(out=T2[:, 2, 8:521], in0=T1[:, 3, 4:517], in1=T1[:, 3, 12:525], op=ALU.add),
                 r=["T1"], w=["T2"])
            srcs = [T1[:, 0, :], T2[:, 0, :], T1[:, 2, :], T2[:, 2, :]]
            for g in range(4):
                w_ = 2 << g
                P.op("dve", lambda e, g=g, w_=w_: e.scalar_tensor_tensor(
                    out=dT[:, g, :], in0=srcs[g][:, 8:520], scalar=1.0 / w_, in1=U[:, g, 8:520],
                    op0=ALU.mult, op1=ALU.subtract),
                    r=["T1", "T2", ("Ub", s)], w=[("dT", g)])
            if b == 0 or b == 7:
                c0 = 0 if b == 0 else 504
                inv = invF if b == 0 else invL
                for g in range(4):
                    P.op("dve", lambda e, g=g: e.tensor_tensor(out=tmp8[:, g, :], in0=srcs[g][:, 8 + c0:16 + c0],
                                                               in1=inv[:, g, :], op=ALU.mult),
                         r=["T1", "T2", "invF", "invL"], w=[("tmp8", g)])
                    P.op("dve", lambda e, g=g: e.tensor_tensor(out=dT[:, g, c0:c0 + 8], in0=tmp8[:, g, :],
                                                               in1=U[:, g, 8 + c0:16 + c0], op=ALU.subtract),
                         r=[("tmp8", g), ("Ub", s)], w=[("dT", g)])
        def pooling_mm(idx):
            for g in range(4):
                bk = next_gbank()
                pb = bank(bk)
                P.op("pe", lambda e, g=g, pb=pb: e.matmul(pb, wpool[:, g, :], dT[:, g, :], start=True, stop=True),
                     r=["wpool", ("dT", g)], w=[("bank", bk)])
                P.op("dve", lambda e, g=g, pb=pb: e.tensor_scalar(mixT[:, 4 + g, :], pb, psc[:, g:g + 1], None, ALU.mult),
                     r=[("bank", bk), "psc"], w=[("mixT", 4 + g)])

        pend_tr = [None]
        pend_atr = []

        def flush_atr():
            for f in pend_atr:
                f()
            del pend_atr[:]

        def attention(idx):
            for jp in range(2):
                rowpair2(idx, 2 * jp, 2 * jp + 1)
                if jp == 0:
                    pooling_mm(idx)
            flush_atr()

        class RP:
            pass

        def rowpair2(idx, jA, jB):
            sq_, b = blocks[idx]
            s = idx % 2
            ws_tile = 4 * b - 2
            rps = []
            for slot, j in enumerate((jA, jB)):
                rp = RP()
                rp.slot = slot
                rp.j = j
                rp.r = 8 * b + 2 * j
                rp.ks = min(max(rp.r - 4, 0), 56)
                rp.nt = 5 if 4 <= rp.r <= 58 else 4
                var = {0: 1, 2: 2, 60: 3, 62: 4}.get(rp.r, 0)
                rp.tab = tabI if var == 0 else tabBs[var]
                rp.tabname = "tabI" if var == 0 else ("tabB", var)
                rp.W = rp.nt * 128
                rp.qs = slice(j * 128, (j + 1) * 128)
                rp.S = Sps[slot]
                rp.O = Ops[slot]
                rps.append(rp)

            def qk(rp, h):
                hp = h // 2
                Q = QA[s] if h % 2 == 0 else QB[s]
                Qn = "QA" if h % 2 == 0 else "QB"
                for t in range(rp.nt):
                    jt = rp.ks // 2 + t - ws_tile
                    P.op("pe", lambda e, t=t, jt=jt: e.matmul(rp.S[:, t * 128:(t + 1) * 128],
                                                            KTb[s][:, hp, jt * 128:(jt + 1) * 128],
                                                            Q[:, hp, rp.qs], start=True, stop=False),
                         r=[("KTb", s), (Qn, s)], w=[("S", rp.slot)])
                    P.op("pe", lambda e, t=t: e.matmul(rp.S[:, t * 128:(t + 1) * 128], ident[:],
                                                      rp.tab[:, h, t * 128:(t + 1) * 128], start=False, stop=True),
                         r=["ident", rp.tabname], w=[("S", rp.slot)])

            def softmax(rp, h):
                pbuf = Pb[rp.slot * 2 + h % 2]
                P.op("act", lambda e: e.activation(out=pbuf[:, 0:rp.nt, :].rearrange("p t q -> p (t q)"),
                                                   in_=rp.S[:, 0:rp.W], func=AF.Exp),
                     r=[("S", rp.slot)], w=[("Pb", rp.slot, h % 2)])

            def pv(rp, h):
                pbuf = Pb[rp.slot * 2 + h % 2]
                for t in range(rp.nt):
                    jt = rp.ks // 2 + t - ws_tile
                    P.op("pe", lambda e, t=t, jt=jt: e.matmul(rp.O[:, h % 4, :], pbuf[:, t, :], Vb[s][:, jt, h, :],
                                                            start=(t == 0), stop=(t == rp.nt - 1)),
                         r=[("Pb", rp.slot, h % 2), ("Vb", s)], w=[("bank", 4 + rp.slot)])

            def norm_half(rp, hh):
                rcb = rc[rp.slot]
                anb = an[rp.slot]
                P.op("dve", lambda e: e.reciprocal(rcb[:], rp.O[:, :, 64]),
                     r=[("bank", 4 + rp.slot)], w=[("rc", rp.slot)])
                P.op("dve", lambda e: e.tensor_tensor(
                    out=anb[:, hh * 256:(hh + 1) * 256].rearrange("p (h d) -> p h d", h=4),
                    in0=rp.O[:, :, 0:64],
                    in1=rcb[:, :].unsqueeze(2).to_broadcast([128, 4, 64]), op=ALU.mult),
                    r=[("bank", 4 + rp.slot), ("rc", rp.slot)], w=[("an", rp.slot, hh)])

            def make_atr(rp):
                def atr():
                    bk = next_gbank()
                    tp = bank_bf(bk)
                    anb = an[rp.slot]
                    for c in range(4):
                        P.op("pe", lambda e, c=c: e.transpose(tp[:, c * 128:(c + 1) * 128], anb[:, c * 128:(c + 1) * 128], ident[:]),
                             r=[("an", rp.slot, 0), ("an", rp.slot, 1), "ident"], w=[("bank", bk)])
                    P.op("dve", lambda e: e.tensor_copy(mixT[:, 0:4, rp.qs], tp[:, 0:512].rearrange("p (c q) -> p c q", c=4)),
                         r=[("bank", bk)], w=[("mixT", 0), ("mixT", 1), ("mixT", 2), ("mixT", 3)])
                return atr

            for h in range(9):
                if h < 8:
                    for rp in rps:
                        qk(rp, h)
                        softmax(rp, h)
                if h == 0:
                    flush_atr()
                if h == 3 and pend_tr[0] is not None:
                    pend_tr[0]()
                    pend_tr[0] = None
                if h >= 1:
                    for rp in rps:
                        pv(rp, h - 1)
                        if h - 1 == 3:
                            norm_half(rp, 0)
                        if h - 1 == 7:
                            norm_half(rp, 1)
            for rp in rps:
                pend_atr.append(make_atr(rp))

        def load_xt(idx, t):
            sq_, b = blocks[idx]
            k = xcnt[0] % 3
            xcnt[0] += 1
            tok = sq_ * SEQ + 512 * b + 128 * t
            P.op("sp", lambda e: e.dma_start(out=xt[k][:], in_=x_d[tok:tok + 128, :]), w=[("xt", k)], dma=f"xt_{k}")
            return k

        def wout_tile(idx, t, k):
            sq_, b = blocks[idx]
            s = idx % 2
            tok = sq_ * SEQ + 512 * b + 128 * t
            for hf in range(2):
                bk = next_gbank()
                pb = bank(bk)
                for kc in range(8):
                    P.op("pe", lambda e, kc=kc, pb=pb, hf=hf: e.matmul(pb, mixT[:, kc, t * 128:(t + 1) * 128],
                                                             wout[:, kc, hf * 512:(hf + 1) * 512],
                                                             start=(kc == 0), stop=(kc == 7)),
                         r=[("mixT", kc), "wout"], w=[("bank", bk)])
                P.op("dve", lambda e, pb=pb, hf=hf: e.tensor_tensor(out=xt[k][:, hf * 512:(hf + 1) * 512], in0=pb,
                                                                   in1=xt[k][:, hf * 512:(hf + 1) * 512], op=ALU.add),
                     r=[("bank", bk)], w=[("xt", k)])
            P.op("sp", lambda e: e.dma_start(out=X2_d[tok:tok + 128, :], in_=xt[k][:]),
                 r=[("xt", k)], w=[("dram", "x2", tok)], dma=f"xt_{k}")
            j = t % 2
            P.op("dve", lambda e: e.bn_stats(st6[j][:, 0, :], xt[k][:, 0:512]), r=[("xt", k)], w=[("st6", j, 0)])
            P.op("dve", lambda e: e.bn_stats(st6[j][:, 1, :], xt[k][:, 512:1024]), r=[("xt", k)], w=[("st6", j, 1)])
            P.op("dve", lambda e: e.bn_aggr(mv[j][:], st6[j][:].rearrange("p a b -> p (a b)")),
                 r=[("st6", j, 0), ("st6", j, 1)], w=[("mv", j)])
            P.op("dve", lambda e: e.scalar_tensor_tensor(out=ss2[j][:], in0=mv[j][:, 0:1], scalar=mv[j][:, 0:1],
                                                         in1=mv[j][:, 1:2], op0=ALU.mult, op1=ALU.add),
                 r=[("mv", j)], w=[("ss2", j)])
            P.op("act", lambda e: e.activation(out=rt2[j][:], in_=ss2[j][:], func=AF.Ln, bias=EPS, scale=1.0),
                 r=[("ss2", j)], w=[("rt2", j)])
            P.op("act", lambda e: e.activation(out=rstd2[j][:], in_=rt2[j][:], func=AF.Exp, scale=-0.5),
                 r=[("rt2", j)], w=[("rstd2", j)])
            P.op("act", lambda e: e.activation(out=h2b[j][:], in_=xt[k][:], func=AF.Copy, scale=rstd2[j][:, 0:1]),
                 r=[("xt", k), ("rstd2", j)], w=[("h2b", j)])
            def deferred():
                bk = next_gbank()
                tp = bank_bf(bk)
                for c in range(8):
                    P.op("pe", lambda e, c=c: e.transpose(tp[:, c * 128:(c + 1) * 128], h2b[j][:, c * 128:(c + 1) * 128], ident[:]),
                         r=[("h2b", j), "ident"], w=[("bank", bk)])
                P.op("dve", lambda e: e.tensor_tensor(
                    out=h2T[s][:, :, t * 128:(t + 1) * 128], in0=tp.rearrange("p (c q) -> p c q", c=8),
                    in1=g2T[:, :].unsqueeze(2).to_broadcast([128, 8, 128]), op=ALU.mult),
                    r=[("bank", bk), "g2T"], w=[("h2T", s, t)])
                if t == 3:
                    n = idx
                    P.op("sp", lambda e: e.dma_start(out=H2T_d[:, :, n * 512:(n + 1) * 512].rearrange("c p t -> p c t"),
                                                     in_=h2T[s][:]),
                         r=[("h2T", s, tt) for tt in range(4)], w=[("dram", "h2T", n)], dma=f"h2T_{s}")
            return deferred

        loads(0)
        for idx in range(len(blocks)):
            if idx + 1 < len(blocks):
                loads(idx + 1)
            ks_ = [load_xt(idx, 0), load_xt(idx, 1)]
            if "pool" not in SKIP:
                pooling(idx)
            if "attn" not in SKIP:
                attention(idx)
            wide[0] = True
            for t in range(4):
                if t + 2 < 4:
                    ks_.append(load_xt(idx, t + 2))
                d_ = wout_tile(idx, t, ks_[t])
                if pend_tr[0] is not None:
                    pend_tr[0]()
                pend_tr[0] = d_
            wide[0] = False
        if pend_tr[0] is not None:
            pend_tr[0]()

    def phase3():
        wg = A.alloc("wg", [128, 8, DFF], BF16)
        wu = A.alloc("wu", [128, 8, DFF], BF16)
        wd = A.alloc("wd", [128, NFC, 1024], BF16)
        h2T = [A.alloc("h2Tb", [128, 8, 512], BF16) for _ in range(2)]
        x2t = [A.alloc("x2t", [128, 1024], F32) for _ in range(2)]
        actT = [A.alloc("actT", [128, NFC, 512], BF16) for _ in range(2)]
        sg = [A.alloc("sg", [128, 512], F32) for _ in range(2)]

        wg_v = w_gate_d.rearrange("(kc p) n -> p kc n", p=128)
        wu_v = w_up_d.rearrange("(kc p) n -> p kc n", p=128)
        wd_v = w_down_d.rearrange("(f p) n -> p f n", p=128)
        GB = [0, 128, 256, 384, 512, 704, 1408, 2112, DFF]
        for g in range(len(GB) - 1):
            c0, c1 = GB[g], GB[g + 1]
            P.op("pool", lambda e, c0=c0, c1=c1: e.dma_start(out=wg[:, :, c0:c1], in_=wg_v[:, :, c0:c1]),
                 w=[("wg", g)], dma=f"wg{g}")
            P.op("pool", lambda e, c0=c0, c1=c1: e.dma_start(out=wu[:, :, c0:c1], in_=wu_v[:, :, c0:c1]),
                 w=[("wu", g)], dma=f"wu{g}")
        for g in range(2):
            P.op("pool", lambda e, g=g: e.dma_start(out=wd[:, g * 11:(g + 1) * 11, :], in_=wd_v[:, g * 11:(g + 1) * 11, :]),
                 w=[("wd", g)], dma=f"wd{g}")

        def wgrp(f):
            lo, hi = f * 128, f * 128 + 128
            return [g for g in range(len(GB) - 1) if GB[g] < hi and GB[g + 1] > lo]

        def load_h(n):
            s = n % 2
            P.op("sp", lambda e: e.dma_start(out=h2T[s][:], in_=H2T_d[:, :, n * 512:(n + 1) * 512].rearrange("c p t -> p c t")),
                 r=[("dram", "h2T", n)], w=[("h2Tb", s)], dma=f"h2Tb_{s}")

        xc = [0]

        def load_x2(n, t):
            k = xc[0] % 2
            xc[0] += 1
            tok = n * 512 + t * 128
            P.op("sp", lambda e: e.dma_start(out=x2t[k][:], in_=X2_d[tok:tok + 128, :]),
                 r=[("dram", "x2", tok)], w=[("x2t", k)], dma=f"x2t_{k}")
            return k

        gu = [0]
        dn = [0]

        def gate_up(n, f, a):
            s = n % 2
            i3 = gu[0] % 3
            gu[0] += 1
            bg = 2 * i3
            bu = 2 * i3 + 1
            pg = bank(bg)
            pu = bank(bu)
            for kc in range(8):
                P.op("pe", lambda e, kc=kc: e.matmul(pg, wg[:, kc, f * 128:(f + 1) * 128], h2T[s][:, kc, :],
                                                    start=(kc == 0), stop=(kc == 7)),
                     r=[("wg", g) for g in wgrp(f)] + [("h2Tb", s)], w=[("bank", bg)])
            for kc in range(8):
                P.op("pe", lambda e, kc=kc: e.matmul(pu, wu[:, kc, f * 128:(f + 1) * 128], h2T[s][:, kc, :],
                                                    start=(kc == 0), stop=(kc == 7)),
                     r=[("wu", g) for g in wgrp(f)] + [("h2Tb", s)], w=[("bank", bu)])
            j = gu[0] % 2
            P.op("act", lambda e: e.activation(out=sg[j][:], in_=pg, func=AF.Silu),
                 r=[("bank", bg)], w=[("sg", j)])
            P.op("dve", lambda e: e.tensor_tensor(out=actT[a][:, f, :], in0=pu, in1=sg[j][:], op=ALU.mult),
                 r=[("bank", bu), ("sg", j)], w=[("actT", a, f)])

        def down(n, a):
            ks_ = {0: load_x2(n, 0), 1: load_x2(n, 1)}
            for t in range(4):
                k = ks_[t]
                for hf in range(2):
                    bk = 6 + (dn[0] % 2)
                    dn[0] += 1
                    pb = bank(bk)
                    for f in range(NFC):
                        P.op("pe", lambda e, f=f, pb=pb, hf=hf, t=t: e.matmul(pb, actT[a][:, f, t * 128:(t + 1) * 128],
                                                                             wd[:, f, hf * 512:(hf + 1) * 512],
                                                                             start=(f == 0), stop=(f == NFC - 1)),
                             r=[("actT", a, f), ("wd", f // 11)], w=[("bank", bk)])
                    P.op("dve", lambda e, pb=pb, hf=hf, k=k: e.tensor_tensor(out=x2t[k][:, hf * 512:(hf + 1) * 512], in0=pb,
                                                                             in1=x2t[k][:, hf * 512:(hf + 1) * 512], op=ALU.add),
                         r=[("bank", bk)], w=[("x2t", k)])
                tok = n * 512 + t * 128
                P.op("sp", lambda e, k=k, tok=tok: e.dma_start(out=out_d[tok:tok + 128, :], in_=x2t[k][:]),
                     r=[("x2t", k)], w=[("dram", "out", tok)], dma=f"x2t_{k}")
                if t + 2 < 4:
                    ks_[t + 2] = load_x2(n, t + 2)

        load_h(0)
        load_h(1)
        for f in range(NFC):
            gate_up(0, f, 0)
            gate_up(1, f, 1)
        load_h(2)
        down(0, 0)
        down(1, 1)
        for n in range(2, NBLK):
            if n + 1 < NBLK:
                load_h(n + 1)
            for f in range(NFC):
                gate_up(n, f, n % 2)
            down(n, n % 2)

    if 1 in phases:
        phase1()
    P.barrier()
    A.reset(phase_mark)
    if 2 in phases:
        phase2()
    P.barrier()
    A.reset(phase_mark)
    A.top = top_save
    if 3 in phases:
        phase3()
    P.final_wait("sp")
    P.emit(nc)
    return nc


def _tab_index():
    idx = np.zeros((5, 128, 640), dtype=np.int64)
    PAD = 15 * 31
    p = np.arange(128)
    qi = np.arange(128)
    for v, r in enumerate([8, 0, 2, 60, 62]):
        ks = min(max(r - 4, 0), 56)
        nt = 5 if 4 <= r <= 58 else 4
        for t in range(5):
            kr = ks + 2 * t + p[:, None] // 64
            kc = p[:, None] % 64
            rq = r + qi[None, :] // 64
            qc = qi[None, :] % 64
            rs = np.clip(rq - 4, 0, 56)
            cs = np.clip(qc - 8, 0, 48)
            valid = (kr >= rs) & (kr < rs + 8) & (kc >= cs) & (kc < cs + 16) & (t < nt)
            ii = (kr - rq + 7) * 31 + (kc - qc + 15)
            idx[v, :, t * 128:(t + 1) * 128] = np.where(valid, ii, PAD)
    return idx


def _inv_tables():
    invF = np.zeros((4, 8), np.float32)
    invL = np.zeros((4, 8), np.float32)
    for g, w in enumerate((2, 4, 8, 16)):
        for j in range(8):
            t = j
            cnt = min(t + w // 2, SEQ) - max(t - w // 2, 0)
            invF[g, j] = 1.0 / cnt
            t = SEQ - 8 + j
            cnt = min(t + w // 2, SEQ) - max(t - w // 2, 0)
            invL[g, j] = 1.0 / cnt
    return (np.ascontiguousarray(np.broadcast_to(invF[None], (128, 4, 8))),
            np.ascontiguousarray(np.broadcast_to(invL[None], (128, 4, 8))))


def make_in_maps(x, norm1_g, w_in, q_norm_g, k_norm_g, rpb, w_pool, pool_scale, w_out,
                 norm2_g, w_gate, w_up, w_down):
    f = lambda a: np.ascontiguousarray(np.asarray(a, dtype=np.float32))
    x = f(x).reshape(N_CORES, TOK, D)
    rpb_pad = np.concatenate([f(rpb)[0].reshape(8, 15 * 31), np.full((8, 1), NEG, np.float32)], axis=1)
    idx = _tab_index()
    tab = np.ascontiguousarray(rpb_pad[:, idx].transpose(1, 2, 0, 3))
    invF, invL = _inv_tables()
    bdm = np.zeros((128, 128), np.float32)
    bdm[:64, :64] = 1.0
    bdm[64:, 64:] = 1.0
    common = dict(
        w_in=f(w_in)[0], w_out=f(w_out)[0], w_gate=f(w_gate)[0], w_up=f(w_up)[0], w_down=f(w_down)[0],
        w_pool=f(w_pool)[0],
        g1T=np.ascontiguousarray(f(norm1_g)[0].reshape(8, 128).T),
        g2T=np.ascontiguousarray(f(norm2_g)[0].reshape(8, 128).T),
        gq=np.ascontiguousarray(np.tile(f(q_norm_g)[0], 2).reshape(128, 1)),
        gk=np.ascontiguousarray(np.tile(f(k_norm_g)[0], 2).reshape(128, 1)),
        pscale=np.ascontiguousarray(f(pool_scale)[0].reshape(4, 128).T),
        tab=tab, ident=np.eye(128, dtype=np.float32), bd=bdm, invF=invF, invL=invL,
    )
    return [dict(common, x=np.ascontiguousarray(x[c])) for c in range(N_CORES)]


_NC_CACHE = {}


def kernel(x, norm1_g, w_in, q_norm_g, k_norm_g, rpb, w_pool, pool_scale, w_out,
           norm2_g, w_gate, w_up, w_down):
    in_maps = make_in_maps(x, norm1_g, w_in, q_norm_g, k_norm_g, rpb, w_pool, pool_scale, w_out,
                           norm2_g, w_gate, w_up, w_down)
    if "nc" not in _NC_CACHE:
        _NC_CACHE["nc"] = build_nc()
    nc = _NC_CACHE["nc"]
    res = run_bass_kernel_spmd(nc, in_maps, core_ids=list(range(N_CORES)))
    out = np.stack([np.asarray(r["out"], dtype=np.float32) for r in res.results], axis=0)
    return out.reshape(16, SEQ, D)
```

```python
import numpy as np
import concourse.bass as bass
import concourse.mybir as mybir
from concourse.bass_utils import run_bass_kernel_spmd

F32 = mybir.dt.float32
BF16 = mybir.dt.bfloat16
ALU = mybir.AluOpType
AF = mybir.ActivationFunctionType
AX = mybir.AxisListType

N_CORES = 8
TOK = 8192
SEQ = 4096
D = 1024
NBLK = 16
DFF = 2816
NFC = 22
EPS = 1e-6
NEG = -30000.0
SEM_ROT = 30000
import os
SKIP = set(os.environ.get('KSKIP', '').split(','))


class Prog:
    ENG = ("sp", "act", "pe", "dve", "pool")

    def __init__(self):
        self.ops = []
        self.tiles = {}
        self.dcount = {}
        self.last = {e: None for e in self.ENG}
        self.bar = {e: None for e in self.ENG}

    def op(self, eng, fn, r=(), w=(), dma=None):
        i = len(self.ops)
        deps = set()
        ddeps = {}

        def add(j):
            o = self.ops[j]
            if o["dma"] is not None:
                c = o["dma"]
                ddeps[c] = max(ddeps.get(c, 0), self.dcount[c])
            else:
                deps.add(j)

        for t in r:
            st = self.tiles.setdefault(t, [None, []])
            if st[0] is not None:
                add(st[0])
        for t in w:
            st = self.tiles.setdefault(t, [None, []])
            if st[0] is not None:
                add(st[0])
            for j in st[1]:
                add(j)
        for t in r:
            self.tiles[t][1].append(i)
        for t in w:
            st = self.tiles[t]
            st[0] = i
            st[1] = []
        if self.bar[eng] is not None:
            bd, bdd = self.bar[eng]
            deps |= bd
            for c, v in bdd.items():
                ddeps[c] = max(ddeps.get(c, 0), v)
            self.bar[eng] = None
        latest = {}
        for j in deps:
            en = self.ops[j]["eng"]
            if en not in latest or j > latest[en]:
                latest[en] = j
        deps = set(latest.values())
        if dma is not None:
            self.dcount[dma] = self.dcount.get(dma, 0) + 1
        self.ops.append(dict(eng=eng, fn=fn, deps=deps, ddeps=ddeps, dma=dma))
        self.last[eng] = i
        return i

    def barrier(self):
        bd = set(j for j in self.last.values() if j is not None and self.ops[j]["dma"] is None)
        for e in self.ENG:
            for j in range(len(self.ops) - 1, -1, -1):
                if self.ops[j]["eng"] == e and self.ops[j]["dma"] is None and self.ops[j]["fn"] is not None:
                    bd.add(j)
                    break
        bdd = dict(self.dcount)
        for e in self.ENG:
            self.bar[e] = (set(bd), dict(bdd))

    def final_wait(self, eng="sp"):
        self.bar[eng] = (set(), dict(self.dcount))
        self.op(eng, None)

    def emit(self, nc):
        ops = self.ops
        need = [False] * len(ops)
        for o in ops:
            for j in o["deps"]:
                if not (o["eng"] == "pe" and ops[j]["eng"] == "pe"):
                    need[j] = True
        cnt = {e: 0 for e in self.ENG}
        sig = [None] * len(ops)
        for i, o in enumerate(ops):
            if need[i]:
                c = cnt[o["eng"]]
                sig[i] = (o["eng"], c // SEM_ROT, c % SEM_ROT + 1)
                cnt[o["eng"]] = c + 1
        esem = {}
        for e in self.ENG:
            for k in range(cnt[e] // SEM_ROT + 1):
                esem[(e, k)] = nc.alloc_semaphore(f"s_{e}_{k}")
        dsem = {c: nc.alloc_semaphore(f"d_{c}") for c in self.dcount}
        for c, v in self.dcount.items():
            assert v * 16 < 60000, (c, v)
        self.n_sems = len(esem) + len(dsem)

        def run(engname):
            def body(e):
                waited = {}
                for i, o in enumerate(ops):
                    if o["eng"] != engname:
                        continue
                    waits = {}
                    for j in o["deps"]:
                        if engname == "pe" and ops[j]["eng"] == "pe":
                            continue
                        en, k, v = sig[j]
                        key = ("e", en, k)
                        waits[key] = max(waits.get(key, 0), v)
                    for c, v in o["ddeps"].items():
                        key = ("d", c)
                        waits[key] = max(waits.get(key, 0), 16 * v)
                    for key, v in waits.items():
                        if waited.get(key, 0) >= v:
                            continue
                        if key[0] == "e":
                            later = [kk for kk in waited if kk[0] == "e" and kk[1] == key[1] and kk[2] > key[2]]
                            if later:
                                continue
                            e.wait_ge(esem[(key[1], key[2])], v)
                        else:
                            e.wait_ge(dsem[key[1]], v)
                        waited[key] = v
                    if o["fn"] is None:
                        continue
                    ins = o["fn"](e)
                    if o["dma"] is not None:
                        ins.then_inc(dsem[o["dma"]], 16)
                    elif sig[i] is not None:
                        ins.then_inc(esem[(sig[i][0], sig[i][1])], 1)
            return body

        with nc.Block() as block:
            block.sync(run("sp"))
            block.scalar(run("act"))
            block.tensor(run("pe"))
            block.vector(run("dve"))
            block.gpsimd(run("pool"))


class Arena:
    def __init__(self, nc):
        self.nc = nc
        self.base = (nc.sbuf_base + 63) // 64 * 64
        self.top = nc.sbuf_top
        self.cur = self.base
        self.n = 0

    def alloc(self, name, shape, dt):
        sz = int(np.prod(shape[1:])) * (2 if dt == BF16 else 4)
        sz = (sz + 63) // 64 * 64
        off = self.cur
        assert off + sz <= self.top, (name, off, sz, self.top)
        self.cur += sz
        self.n += 1
        return self.nc.alloc_sbuf_tensor_at(f"{name}_{self.n}", list(shape), dt, offset=off)

    def alloc_top(self, name, shape, dt):
        sz = int(np.prod(shape[1:])) * (2 if dt == BF16 else 4)
        sz = (sz + 63) // 64 * 64
        self.top -= sz
        self.top = self.top // 64 * 64
        assert self.top >= self.cur, (name, self.top, self.cur)
        self.n += 1
        return self.nc.alloc_sbuf_tensor_at(f"{name}_{self.n}", list(shape), dt, offset=self.top)

    def mark(self):
        return self.cur

    def reset(self, m):
        self.cur = m


def build_nc(debug=False, phases=(1, 2, 3)):
    nc = bass.Bass("TRN2", target_bir_lowering=False)
    P = Prog()
    A = Arena(nc)

    def din(name, shape, dt=F32):
        return nc.dram_tensor(name, list(shape), dt, kind="ExternalInput").ap()

    x_d = din("x", [TOK, D])
    w_in_d = din("w_in", [D, 2048])
    w_out_d = din("w_out", [D, D])
    w_gate_d = din("w_gate", [D, DFF])
    w_up_d = din("w_up", [D, DFF])
    w_down_d = din("w_down", [DFF, D])
    w_pool_d = din("w_pool", [4, 128, 128])
    g1T_d = din("g1T", [128, 8])
    g2T_d = din("g2T", [128, 8])
    gq_d = din("gq", [128, 1])
    gk_d = din("gk", [128, 1])
    psc_d = din("pscale", [128, 4])
    tab_d = din("tab", [5, 128, 8, 640])
    ident_d = din("ident", [128, 128])
    bd_d = din("bd", [128, 128])
    invF_d = din("invF", [128, 4, 8])
    invL_d = din("invL", [128, 4, 8])
    out_d = nc.dram_tensor("out", [TOK, D], F32, kind="ExternalOutput").ap()

    skind = "ExternalOutput" if debug else "Internal"
    QT_d = nc.dram_tensor("QT", [4, 128, TOK], BF16, kind=skind).ap()
    KT_d = nc.dram_tensor("KT", [4, 128, TOK], BF16, kind=skind).ap()
    V_d = nc.dram_tensor("V", [TOK, 520], BF16, kind=skind).ap()
    UT_d = nc.dram_tensor("UT", [4, 128, TOK], F32, kind=skind).ap()
    X2_d = nc.dram_tensor("X2", [TOK, D], F32, kind=skind).ap()
    H2T_d = nc.dram_tensor("H2T", [8, 128, TOK], BF16, kind=skind).ap()

    PS = nc.alloc_psum_tensor("ps", [128, 4096], F32)

    def bank(k, n=1):
        return PS[:, 512 * k:512 * (k + n)]

    def bank_bf(k):
        return PS[:, 512 * k:512 * (k + 1)].bitcast(BF16)

    ident = A.alloc("ident", [128, 128], BF16)
    bd = A.alloc("bd", [128, 128], BF16)
    g1T = A.alloc("g1T", [128, 8], F32)
    g2T = A.alloc("g2T", [128, 8], F32)
    gq = A.alloc("gq", [128, 1], F32)
    gk = A.alloc("gk", [128, 1], F32)
    gk8 = A.alloc("gk8", [128, 1], F32)
    psc = A.alloc("psc", [128, 4], F32)
    P.op("pool", lambda e: e.dma_start(out=ident[:], in_=ident_d), w=["ident"], dma="c_ident")
    P.op("pool", lambda e: e.dma_start(out=bd[:], in_=bd_d), w=["bd"], dma="c_bd")
    P.op("sp", lambda e: e.dma_start(out=g1T[:], in_=g1T_d), w=["g1T"], dma="c_g1")
    P.op("sp", lambda e: e.dma_start(out=g2T[:], in_=g2T_d), w=["g2T"], dma="c_g2")
    P.op("sp", lambda e: e.dma_start(out=gq[:], in_=gq_d), w=["gq"], dma="c_gq")
    P.op("sp", lambda e: e.dma_start(out=gk[:], in_=gk_d), w=["gk"], dma="c_gk")
    P.op("sp", lambda e: e.dma_start(out=psc[:], in_=psc_d), w=["psc"], dma="c_psc")
    P.op("dve", lambda e: e.tensor_scalar(gk8[:], gk[:], 8.0, None, ALU.mult), r=["gk"], w=["gk8"])
    phase_mark = A.mark()
    top_save = A.top
    tabI = A.alloc_top("tabI", [128, 8, 640], BF16)
    tabBs = [None] + [A.alloc_top("tabB", [128, 8, 512], BF16) for _ in range(4)]
    wpool = A.alloc_top("wpool", [128, 4, 128], BF16)
    top_p12 = A.top

    def early_phase2_loads():
        for v in (1, 2):
            P.op("pool", lambda e, v=v: e.dma_start(out=tabBs[v][:], in_=tab_d[v][:, :, 0:512]),
                 w=[("tabB", v)], dma=f"tabB{v}")
        P.op("pool", lambda e: e.dma_start(out=tabI[:], in_=tab_d[0]), w=["tabI"], dma="tabI")
        P.op("pool", lambda e: e.dma_start(out=wpool[:], in_=w_pool_d.rearrange("g c d -> c g d")),
             w=["wpool"], dma="wpool")
        for v in (3, 4):
            P.op("pool", lambda e, v=v: e.dma_start(out=tabBs[v][:], in_=tab_d[v][:, :, 0:512]),
                 w=[("tabB", v)], dma=f"tabB{v}")

    def phase1():
        win = A.alloc("win", [128, 8, 2048], BF16)
        x1 = [A.alloc("x1", [128, 4, 1024], F32) for _ in range(2)]
        sqx = A.alloc("sqx", [128, 4, 1024], F32)
        ss = A.alloc("ss", [128, 4], F32)
        rt = A.alloc("rt", [128, 4], F32)
        rstd = A.alloc("rstd", [128, 4], F32)
        hb = [A.alloc("hb", [128, 1024], BF16) for _ in range(4)]
        hT = [A.alloc("hT", [128, 8, 512], BF16) for _ in range(2)]
        sq = [A.alloc("sq", [128, 512], BF16) for _ in range(2)]
        rs = [A.alloc("rs", [128, 512], F32) for _ in range(2)]
        rr = [A.alloc("rr", [128, 512], F32) for _ in range(2)]
        qst = [A.alloc("qst", [128, 4, 512], BF16) for _ in range(2)]
        kst = [A.alloc("kst", [128, 4, 512], BF16) for _ in range(2)]
        vst = [A.alloc("vst", [128, 4, 8, 65], BF16) for _ in range(2)]
        ust = [A.alloc("ust", [128, 4, 512], F32) for _ in range(2)]

        w_in_v = w_in_d.rearrange("(kc p) n -> p kc n", p=128)
        for g in range(4):
            P.op("pool", lambda e, g=g: e.dma_start(out=win[:, :, g * 512:(g + 1) * 512],
                                                    in_=w_in_v[:, :, g * 512:(g + 1) * 512]),
                 w=[("win", g)], dma=f"win{g}")
        for s in range(2):
            P.op("pool", lambda e, s=s: e.memset(vst[s][:], 1.0), w=[("vst", s)])
        early_phase2_loads()

        def load_x(n):
            s = n % 2
            P.op("sp", lambda e: e.dma_start(
                out=x1[s][:], in_=x_d[n * 512:(n + 1) * 512, :].rearrange("(t p) d -> p t d", p=128)),
                w=[("x1", s, t) for t in range(4)], dma=f"x1_{s}")

        def norm_step(n, k):
            s = n % 2
            if k < 4:
                t = k
                P.op("act", lambda e: e.activation(out=sqx[:, t, :], in_=x1[s][:, t, :], func=AF.Square),
                     r=[("x1", s, t)], w=[("sqx", t)])
            elif k < 8:
                t = k - 4
                P.op("dve", lambda e: e.tensor_reduce(out=ss[:, t:t + 1], in_=sqx[:, t, :], axis=AX.X, op=ALU.add),
                     r=[("sqx", t)], w=[("ss", t)])
            elif k == 8:
                P.op("act", lambda e: e.activation(out=rt[:], in_=ss[:], func=AF.Ln, bias=EPS, scale=1.0 / D),
                     r=[("ss", t) for t in range(4)], w=["rt"])
            elif k == 9:
                P.op("act", lambda e: e.activation(out=rstd[:], in_=rt[:], func=AF.Exp, scale=-0.5), r=["rt"], w=["rstd"])
            elif k < 14:
                t = k - 10
                P.op("act", lambda e: e.activation(out=hb[t][:], in_=x1[s][:, t, :], func=AF.Copy,
                                                   scale=rstd[:, t:t + 1]),
                     r=[("x1", s, t), "rstd"], w=[("hb", t)])
            else:
                t = k - 14
                j = t % 2
                tp = bank_bf(j)
                for c in range(8):
                    P.op("pe", lambda e, c=c: e.transpose(tp[:, c * 128:(c + 1) * 128], hb[t][:, c * 128:(c + 1) * 128], ident[:]),
                         r=[("hb", t), "ident"], w=[("bank", j)])
                P.op("dve", lambda e: e.tensor_tensor(
                    out=hT[s][:, :, t * 128:(t + 1) * 128],
                    in0=tp.rearrange("p (c q) -> p c q", c=8),
                    in1=g1T[:, :].unsqueeze(2).to_broadcast([128, 8, 128]), op=ALU.mult),
                    r=[("bank", j), "g1T"], w=[("hT", s, t)])
        NSTEP = 18
        SCHED = {0: [0, 1], 1: [2, 4], 2: [3, 5], 3: [6], 4: [7], 5: [8], 6: [9], 7: [10], 8: [11], 9: [12, 14],
                 10: [13], 11: [15], 12: [16], 13: [17], 14: [], 15: []}

        mmb = [0]

        def next_bank():
            b = 2 + (mmb[0] % 5)
            mmb[0] += 1
            return b

        smb = [0]

        def qk_A(n, c):
            s = n % 2
            b = next_bank()
            pb = bank(b)
            for kc in range(8):
                P.op("pe", lambda e, kc=kc: e.matmul(pb, win[:, kc, c * 128:(c + 1) * 128], hT[s][:, kc, :],
                                                    start=(kc == 0), stop=(kc == 7)),
                     r=[("win", c // 4)] + [("hT", s, t) for t in range(4)], w=[("bank", b)])
            j = smb[0] % 2
            smb[0] += 1
            P.op("act", lambda e: e.activation(out=sq[j][:], in_=pb, func=AF.Square),
                 r=[("bank", b)], w=[("sq", j)])
            return (n, c, b, j)

        def qk_B(st_):
            n, c, b, j = st_
            s = n % 2
            pb = bank(b)
            isq = c < 4
            st = qst[s] if isq else kst[s]
            stname = "qst" if isq else "kst"
            b2 = 7
            pb2 = bank(b2)
            P.op("pe", lambda e: e.matmul(pb2, bd[:], sq[j][:], start=True, stop=True),
                 r=["bd", ("sq", j)], w=[("bank", b2)])
            P.op("act", lambda e: e.activation(out=rs[j][:], in_=pb2, func=AF.Ln, bias=64.0 * EPS, scale=1.0),
                 r=[("bank", b2)], w=[("rs", j)])
            P.op("act", lambda e: e.activation(out=rr[j][:], in_=rs[j][:], func=AF.Exp, scale=-0.5),
                 r=[("rs", j)], w=[("rr", j)])
            gv = gq if isq else gk8
            P.op("dve", lambda e: e.scalar_tensor_tensor(out=st[:, c % 4, :], in0=pb, scalar=gv[:, 0:1], in1=rr[j][:],
                                                         op0=ALU.mult, op1=ALU.mult),
                 r=[("bank", b), ("rr", j), "gq", "gk8"], w=[(stname, s, c % 4)])
            if c % 4 == 3:
                dst = QT_d if isq else KT_d
                P.op("sp", lambda e: e.dma_start(out=dst[:, :, n * 512:(n + 1) * 512].rearrange("c p t -> p c t"),
                                                 in_=st[:]),
                     r=[(stname, s, cc) for cc in range(4)], w=[("dram", stname, n)], dma=f"{stname}_{s}")

        def v_tile(n, t):
            s = n % 2
            b = next_bank()
            pb = bank(b)
            for kc in range(8):
                P.op("pe", lambda e, kc=kc: e.matmul(pb, hT[s][:, kc, t * 128:(t + 1) * 128], win[:, kc, 1024:1536],
                                                    start=(kc == 0), stop=(kc == 7)),
                     r=[("win", 2), ("hT", s, t)], w=[("bank", b)])
            P.op("dve", lambda e: e.tensor_copy(vst[s][:, t, :, 0:64], pb.rearrange("p (h d) -> p h d", h=8)),
                 r=[("bank", b)], w=[("vst", s)])
            if t == 3:
                P.op("sp", lambda e: e.dma_start(
                    out=V_d[n * 512:(n + 1) * 512, :].rearrange("(t p) e -> p t e", p=128),
                    in_=vst[s][:].rearrange("p t h d -> p t (h d)")),
                    r=[("vst", s)], w=[("dram", "v", n)], dma=f"vst_{s}")

        def u_chunk(n, g):
            s = n % 2
            b = next_bank()
            pb = bank(b)
            c = 12 + g
            for kc in range(8):
                P.op("pe", lambda e, kc=kc: e.matmul(pb, win[:, kc, c * 128:(c + 1) * 128], hT[s][:, kc, :],
                                                    start=(kc == 0), stop=(kc == 7)),
                     r=[("win", 3)] + [("hT", s, t) for t in range(4)], w=[("bank", b)])
            P.op("dve", lambda e: e.tensor_copy(ust[s][:, g, :], pb),
                 r=[("bank", b)], w=[("ust", s, g)])
            if g == 3:
                P.op("sp", lambda e: e.dma_start(out=UT_d[:, :, n * 512:(n + 1) * 512].rearrange("c p t -> p c t"),
                                                 in_=ust[s][:]),
                     r=[("ust", s, gg) for gg in range(4)], w=[("dram", "u", n)], dma=f"ust_{s}")

        load_x(0)
        load_x(1)
        for k in range(NSTEP):
            norm_step(0, k)
        units = [("qk", 0), ("v", 0), ("qk", 1), ("u", 0), ("qk", 2), ("v", 1), ("qk", 3), ("u", 1),
                 ("qk", 4), ("v", 2), ("qk", 5), ("u", 2), ("qk", 6), ("v", 3), ("qk", 7), ("u", 3)]
        for n in range(NBLK):
            pend = None
            if n + 2 < NBLK:
                load_x(n + 2)
            for i, (kind, a) in enumerate(units):
                if kind == "qk":
                    st_ = qk_A(n, a)
                    if pend is not None:
                        qk_B(pend)
                    pend = st_
                elif kind == "v":
                    v_tile(n, a)
                else:
                    u_chunk(n, a)
                if n + 1 < NBLK:
                    for k in SCHED[i]:
                        norm_step(n + 1, k)
            qk_B(pend)

    def phase2():
        wout = A.alloc("wout", [128, 8, 1024], BF16)
        invF = A.alloc("invF", [128, 4, 8], F32)
        invL = A.alloc("invL", [128, 4, 8], F32)
        KTb = [A.alloc("KTb", [128, 4, 1024], BF16) for _ in range(2)]
        Vb = [A.alloc("Vb", [128, 8, 8, 65], BF16) for _ in range(2)]
        QA = [A.alloc("QA", [128, 4, 512], BF16) for _ in range(2)]
        QB = [A.alloc("QB", [128, 4, 512], BF16) for _ in range(2)]
        Ub = [A.alloc("Ub", [128, 4, 528], F32) for _ in range(2)]
        T1 = A.alloc("T1", [128, 4, 528], F32)
        T2 = A.alloc("T2", [128, 3, 528], F32)
        tmp8 = A.alloc("tmp8", [128, 4, 8], F32)
        dT = A.alloc("dT", [128, 4, 512], BF16)
        xt = [A.alloc("xt", [128, 1024], F32) for _ in range(3)]
        mixT = A.alloc("mixT", [128, 8, 512], BF16)
        Pb = [A.alloc("Pb", [128, 5, 128], BF16) for _ in range(4)]
        rc = [A.alloc("rc", [128, 4], F32) for _ in range(2)]
        an = [A.alloc("an", [128, 512], BF16) for _ in range(2)]
        sqx = A.alloc("sqx2", [128, 1024], F32)
        ss2 = [A.alloc("ss2", [128, 1], F32) for _ in range(2)]
        st6 = [A.alloc("st6", [128, 2, 6], F32) for _ in range(2)]
        mv = [A.alloc("mv", [128, 2], F32) for _ in range(2)]
        rt2 = [A.alloc("rt2", [128, 1], F32) for _ in range(2)]
        rstd2 = [A.alloc("rstd2", [128, 1], F32) for _ in range(2)]
        h2b = [A.alloc("h2b", [128, 1024], BF16) for _ in range(2)]
        h2T = [A.alloc("h2T", [128, 8, 512], BF16) for _ in range(2)]

        Sps = [PS[:, 0:640], PS[:, 1024:1664]]
        Ops = [PS[:, 512 * 4:512 * 4 + 260].rearrange("p (h e) -> p h e", h=4),
               PS[:, 512 * 5:512 * 5 + 260].rearrange("p (h e) -> p h e", h=4)]
        gb = [0]

        wide = [False]

        def next_gbank():
            if wide[0]:
                b = 4 + (gb[0] % 4)
            else:
                b = 6 + (gb[0] % 2)
            gb[0] += 1
            return b

        P.op("pool", lambda e: e.dma_start(out=wout[:], in_=w_out_d.rearrange("(kc p) n -> p kc n", p=128)),
             w=["wout"], dma="wout")
        P.op("sp", lambda e: e.dma_start(out=invF[:], in_=invF_d), w=["invF"], dma="c_invF")
        P.op("sp", lambda e: e.dma_start(out=invL[:], in_=invL_d), w=["invL"], dma="c_invL")
        for s in range(2):
            P.op("pool", lambda e, s=s: e.memset(QA[s][:], 0.0), w=[("QA", s)])
            P.op("pool", lambda e, s=s: e.memset(QB[s][:], 0.0), w=[("QB", s)])

        blocks = [(sq_, b) for sq_ in range(2) for b in range(8)]

        def loads(idx):
            sq_, b = blocks[idx]
            s = idx % 2
            T0 = sq_ * SEQ
            ws = 512 * b - 256
            lo = max(0, ws)
            hi = min(SEQ, ws + 1024)
            P.op("sp", lambda e: e.dma_start(out=KTb[s][:, :, lo - ws:hi - ws],
                                             in_=KT_d[:, :, T0 + lo:T0 + hi].rearrange("c p t -> p c t")),
                 r=[("dram", "kst", n) for n in range(NBLK)], w=[("KTb", s)], dma=f"KTb_{s}")
            j0 = (lo - ws) // 128
            j1 = (hi - ws) // 128
            P.op("sp", lambda e: e.dma_start(
                out=Vb[s][:, j0:j1, :, :].rearrange("p j h e -> p j (h e)"),
                in_=V_d[T0 + lo:T0 + hi, :].rearrange("(j p) e -> p j e", p=128)),
                r=[("dram", "v", n) for n in range(NBLK)], w=[("Vb", s)], dma=f"Vb_{s}")
            q0 = T0 + 512 * b
            P.op("sp", lambda e: e.dma_start(out=QA[s][0:64, :, :],
                                             in_=QT_d[:, 0:64, q0:q0 + 512].rearrange("c p t -> p c t")),
                 r=[("dram", "qst", n) for n in range(NBLK)], w=[("QA", s)], dma=f"QA_{s}")
            P.op("sp", lambda e: e.dma_start(out=QB[s][64:128, :, :],
                                             in_=QT_d[:, 64:128, q0:q0 + 512].rearrange("c p t -> p c t")),
                 r=[("dram", "qst", n) for n in range(NBLK)], w=[("QB", s)], dma=f"QB_{s}")
            ulo = max(0, 512 * b - 8)
            uhi = min(SEQ, 512 * b + 520)
            i0 = ulo - (512 * b - 8)
            i1 = uhi - (512 * b - 8)
            if i0 > 0:
                P.op("pool", lambda e: e.memset(Ub[s][:, :, 0:i0], 0.0), w=[("Ub", s)])
            if i1 < 528:
                P.op("pool", lambda e: e.memset(Ub[s][:, :, i1:528], 0.0), w=[("Ub", s)])
            P.op("sp", lambda e: e.dma_start(out=Ub[s][:, :, i0:i1],
                                             in_=UT_d[:, :, T0 + ulo:T0 + uhi].rearrange("c p t -> p c t")),
                 r=[("dram", "u", n) for n in range(NBLK)], w=[("Ub", s)], dma=f"Ub_{s}")

        xcnt = [0]

        def pooling(idx):
            sq_, b = blocks[idx]
            s = idx % 2
            U = Ub[s]
            eng = "pool"
            P.op(eng, lambda e: e.tensor_tensor(out=T1[:, :, 1:528], in0=U[:, :, 0:527], in1=U[:, :, 1:528], op=ALU.add),
                 r=[("Ub", s)], w=["T1"])
            P.op(eng, lambda e: e.tensor_tensor(out=T2[:, 0:3, 2:527], in0=T1[:, 1:4, 1:526], in1=T1[:, 1:4, 3:528], op=ALU.add),
                 r=["T1"], w=["T2"])
            P.op(eng, lambda e: e.tensor_tensor(out=T1[:, 2:4, 4:525], in0=T2[:, 1:3, 2:523], in1=T2[:, 1:3, 6:527], op=ALU.add),
                 r=["T2"], w=["T1"])
            P.op(eng, lambda e: e.tensor_tensor(out=T2[:, 2, 8:521], in0=T1[:, 3, 4:517], in1=T1[:, 3, 12:525], op=ALU.add),
                 r=["T1"], w=["T2"])
            srcs = [T1[:, 0, :], T2[:, 0, :], T1[:, 2, :], T2[:, 2, :]]
            for g in range(4):
                w_ = 2 << g
                P.op("dve", lambda e, g=g, w_=w_: e.scalar_tensor_tensor(
                    out=dT[:, g, :], in0=srcs[g][:, 8:520], scalar=1.0 / w_, in1=U[:, g, 8:520],
                    op0=ALU.mult, op1=ALU.subtract),
                    r=["T1", "T2", ("Ub", s)], w=[("dT", g)])
            if b == 0 or b == 7:
                c0 = 0 if b == 0 else 504
                inv = invF if b == 0 else invL
                for g in range(4):
                    P.op("dve", lambda e, g=g: e.tensor_tensor(out=tmp8[:, g, :], in0=srcs[g][:, 8 + c0:16 + c0],
                                                               in1=inv[:, g, :], op=ALU.mult),
                         r=["T1", "T2", "invF", "invL"], w=[("tmp8", g)])
                    P.op("dve", lambda e, g=g: e.tensor_tensor(out=dT[:, g, c0:c0 + 8], in0=tmp8[:, g, :],
                                                               in1=U[:, g, 8 + c0:16 + c0], op=ALU.subtract),
                         r=[("tmp8", g), ("Ub", s)], w=[("dT", g)])
        def pooling_mm(idx):
            for g in range(4):
                bk = next_gbank()
                pb = bank(bk)
                P.op("pe", lambda e, g=g, pb=pb: e.matmul(pb, wpool[:, g, :], dT[:, g, :], start=True, stop=True),
                     r=["wpool", ("dT", g)], w=[("bank", bk)])
                P.op("dve", lambda e, g=g, pb=pb: e.tensor_scalar(mixT[:, 4 + g, :], pb, psc[:, g:g + 1], None, ALU.mult),
                     r=[("bank", bk), "psc"], w=[("mixT", 4 + g)])

        pend_tr = [None]
        pend_late = [None]
        pend_atr = []

        def flush_atr():
            for f in pend_atr:
                f()
            del pend_atr[:]

        def attention(idx):
            for jp in range(2):
                rowpair2(idx, 2 * jp, 2 * jp + 1)
                if jp == 0:
                    pooling_mm(idx)
            flush_atr()

        class RP:
            pass

        def rowpair2(idx, jA, jB):
            sq_, b = blocks[idx]
            s = idx % 2
            ws_tile = 4 * b - 2
            rps = []
            for slot, j in enumerate((jA, jB)):
                rp = RP()
                rp.slot = slot
                rp.j = j
                rp.r = 8 * b + 2 * j
                rp.ks = min(max(rp.r - 4, 0), 56)
                rp.nt = 5 if 4 <= rp.r <= 58 else 4
                var = {0: 1, 2: 2, 60: 3, 62: 4}.get(rp.r, 0)
                rp.tab = tabI if var == 0 else tabBs[var]
                rp.tabname = "tabI" if var == 0 else ("tabB", var)
                rp.W = rp.nt * 128
                rp.qs = slice(j * 128, (j + 1) * 128)
                rp.S = Sps[slot]
                rp.O = Ops[slot]
                rps.append(rp)

            def qk(rp, h):
                hp = h // 2
                Q = QA[s] if h % 2 == 0 else QB[s]
                Qn = "QA" if h % 2 == 0 else "QB"
                for t in range(rp.nt):
                    jt = rp.ks // 2 + t - ws_tile
                    P.op("pe", lambda e, t=t, jt=jt: e.matmul(rp.S[:, t * 128:(t + 1) * 128],
                                                            KTb[s][:, hp, jt * 128:(jt + 1) * 128],
                                                            Q[:, hp, rp.qs], start=True, stop=False),
                         r=[("KTb", s), (Qn, s)], w=[("S", rp.slot)])
                    P.op("pe", lambda e, t=t: e.matmul(rp.S[:, t * 128:(t + 1) * 128], ident[:],
                                                      rp.tab[:, h, t * 128:(t + 1) * 128], start=False, stop=True),
                         r=["ident", rp.tabname], w=[("S", rp.slot)])

            def softmax(rp, h):
                pbuf = Pb[rp.slot * 2 + h % 2]
                P.op("act", lambda e: e.activation(out=pbuf[:, 0:rp.nt, :].rearrange("p t q -> p (t q)"),
                                                   in_=rp.S[:, 0:rp.W], func=AF.Exp),
                     r=[("S", rp.slot)], w=[("Pb", rp.slot, h % 2)])

            def pv(rp, h):
                pbuf = Pb[rp.slot * 2 + h % 2]
                for t in range(rp.nt):
                    jt = rp.ks // 2 + t - ws_tile
                    P.op("pe", lambda e, t=t, jt=jt: e.matmul(rp.O[:, h % 4, :], pbuf[:, t, :], Vb[s][:, jt, h, :],
                                                            start=(t == 0), stop=(t == rp.nt - 1)),
                         r=[("Pb", rp.slot, h % 2), ("Vb", s)], w=[("bank", 4 + rp.slot)])

            def norm_half(rp, hh):
                rcb = rc[rp.slot]
                anb = an[rp.slot]
                P.op("dve", lambda e: e.reciprocal(rcb[:], rp.O[:, :, 64]),
                     r=[("bank", 4 + rp.slot)], w=[("rc", rp.slot)])
                P.op("dve", lambda e: e.tensor_tensor(
                    out=anb[:, hh * 256:(hh + 1) * 256].rearrange("p (h d) -> p h d", h=4),
                    in0=rp.O[:, :, 0:64],
                    in1=rcb[:, :].unsqueeze(2).to_broadcast([128, 4, 64]), op=ALU.mult),
                    r=[("bank", 4 + rp.slot), ("rc", rp.slot)], w=[("an", rp.slot, hh)])

            def make_atr(rp):
                def atr():
                    bk = next_gbank()
                    tp = bank_bf(bk)
                    anb = an[rp.slot]
                    for c in range(4):
                        P.op("pe", lambda e, c=c: e.transpose(tp[:, c * 128:(c + 1) * 128], anb[:, c * 128:(c + 1) * 128], ident[:]),
                             r=[("an", rp.slot, 0), ("an", rp.slot, 1), "ident"], w=[("bank", bk)])
                    P.op("dve", lambda e: e.tensor_copy(mixT[:, 0:4, rp.qs], tp[:, 0:512].rearrange("p (c q) -> p c q", c=4)),
                         r=[("bank", bk)], w=[("mixT", 0), ("mixT", 1), ("mixT", 2), ("mixT", 3)])
                return atr

            for h in range(9):
                if h < 8:
                    for rp in rps:
                        qk(rp, h)
                        softmax(rp, h)
                if h == 0:
                    if pend_late[0] is not None:
                        pend_late[0]()
                        pend_late[0] = None
                    flush_atr()
                if h == 3 and pend_tr[0] is not None:
                    pend_tr[0]()
                    pend_tr[0] = None
                if h >= 1:
                    for rp in rps:
                        pv(rp, h - 1)
                        if h - 1 == 3:
                            norm_half(rp, 0)
                        if h - 1 == 7:
                            norm_half(rp, 1)
            for rp in rps:
                pend_atr.append(make_atr(rp))

        def load_xt(idx, t):
            sq_, b = blocks[idx]
            k = xcnt[0] % 3
            xcnt[0] += 1
            tok = sq_ * SEQ + 512 * b + 128 * t
            P.op("sp", lambda e: e.dma_start(out=xt[k][:], in_=x_d[tok:tok + 128, :]), w=[("xt", k)], dma=f"xt_{k}")
            return k

        def wout_tile(idx, t, k):
            sq_, b = blocks[idx]
            s = idx % 2
            tok = sq_ * SEQ + 512 * b + 128 * t
            for hf in range(2):
                bk = next_gbank()
                pb = bank(bk)
                for kc in range(8):
                    P.op("pe", lambda e, kc=kc, pb=pb, hf=hf: e.matmul(pb, mixT[:, kc, t * 128:(t + 1) * 128],
                                                             wout[:, kc, hf * 512:(hf + 1) * 512],
                                                             start=(kc == 0), stop=(kc == 7)),
                         r=[("mixT", kc), "wout"], w=[("bank", bk)])
                P.op("dve", lambda e, pb=pb, hf=hf: e.tensor_tensor(out=xt[k][:, hf * 512:(hf + 1) * 512], in0=pb,
                                                                   in1=xt[k][:, hf * 512:(hf + 1) * 512], op=ALU.add),
                     r=[("bank", bk)], w=[("xt", k)])
            P.op("sp", lambda e: e.dma_start(out=X2_d[tok:tok + 128, :], in_=xt[k][:]),
                 r=[("xt", k)], w=[("dram", "x2", tok)], dma=f"xt_{k}")
            j = t % 2
            P.op("dve", lambda e: e.bn_stats(st6[j][:, 0, :], xt[k][:, 0:512]), r=[("xt", k)], w=[("st6", j, 0)])
            P.op("dve", lambda e: e.bn_stats(st6[j][:, 1, :], xt[k][:, 512:1024]), r=[("xt", k)], w=[("st6", j, 1)])
            P.op("dve", lambda e: e.bn_aggr(mv[j][:], st6[j][:].rearrange("p a b -> p (a b)")),
                 r=[("st6", j, 0), ("st6", j, 1)], w=[("mv", j)])
            P.op("dve", lambda e: e.scalar_tensor_tensor(out=ss2[j][:], in0=mv[j][:, 0:1], scalar=mv[j][:, 0:1],
                                                         in1=mv[j][:, 1:2], op0=ALU.mult, op1=ALU.add),
                 r=[("mv", j)], w=[("ss2", j)])
            def act_part():
                P.op("act", lambda e: e.activation(out=rt2[j][:], in_=ss2[j][:], func=AF.Ln, bias=EPS, scale=1.0),
                     r=[("ss2", j)], w=[("rt2", j)])
                P.op("act", lambda e: e.activation(out=rstd2[j][:], in_=rt2[j][:], func=AF.Exp, scale=-0.5),
                     r=[("rt2", j)], w=[("rstd2", j)])
                P.op("act", lambda e: e.activation(out=h2b[j][:], in_=xt[k][:], func=AF.Copy, scale=rstd2[j][:, 0:1]),
                     r=[("xt", k), ("rstd2", j)], w=[("h2b", j)])
            if t == 3:
                pend_late[0] = act_part
            else:
                act_part()
            def deferred():
                bk = next_gbank()
                tp = bank_bf(bk)
                for c in range(8):
                    P.op("pe", lambda e, c=c: e.transpose(tp[:, c * 128:(c + 1) * 128], h2b[j][:, c * 128:(c + 1) * 128], ident[:]),
                         r=[("h2b", j), "ident"], w=[("bank", bk)])
                P.op("dve", lambda e: e.tensor_tensor(
                    out=h2T[s][:, :, t * 128:(t + 1) * 128], in0=tp.rearrange("p (c q) -> p c q", c=8),
                    in1=g2T[:, :].unsqueeze(2).to_broadcast([128, 8, 128]), op=ALU.mult),
                    r=[("bank", bk), "g2T"], w=[("h2T", s, t)])
                if t == 3:
                    n = idx
                    P.op("sp", lambda e: e.dma_start(out=H2T_d[:, :, n * 512:(n + 1) * 512].rearrange("c p t -> p c t"),
                                                     in_=h2T[s][:]),
                         r=[("h2T", s, tt) for tt in range(4)], w=[("dram", "h2T", n)], dma=f"h2T_{s}")
            return deferred

        loads(0)
        for idx in range(len(blocks)):
            if idx + 1 < len(blocks):
                loads(idx + 1)
            ks_ = [load_xt(idx, 0), load_xt(idx, 1)]
            if "pool" not in SKIP:
                pooling(idx)
            if "attn" not in SKIP:
                attention(idx)
            wide[0] = True
            for t in range(4):
                if t + 2 < 4:
                    ks_.append(load_xt(idx, t + 2))
                d_ = wout_tile(idx, t, ks_[t])
                if pend_tr[0] is not None:
                    pend_tr[0]()
                pend_tr[0] = d_
            wide[0] = False
        if pend_late[0] is not None:
            pend_late[0]()
            pend_late[0] = None
        if pend_tr[0] is not None:
            pend_tr[0]()

    def phase3():
        wg = A.alloc("wg", [128, 8, DFF], BF16)
        wu = A.alloc("wu", [128, 8, DFF], BF16)
        wd = A.alloc("wd", [128, NFC, 1024], BF16)
        h2T = [A.alloc("h2Tb", [128, 8, 512], BF16) for _ in range(2)]
        x2t = [A.alloc("x2t", [128, 1024], F32) for _ in range(2)]
        actT = [A.alloc("actT", [128, NFC, 512], BF16) for _ in range(2)]
        sg = [A.alloc("sg", [128, 512], F32) for _ in range(2)]

        wg_v = w_gate_d.rearrange("(kc p) n -> p kc n", p=128)
        wu_v = w_up_d.rearrange("(kc p) n -> p kc n", p=128)
        wd_v = w_down_d.rearrange("(f p) n -> p f n", p=128)
        GB = [0, 128, 256, 384, 512, 704, 1408, 2112, DFF]
        for g in range(len(GB) - 1):
            c0, c1 = GB[g], GB[g + 1]
            P.op("pool", lambda e, c0=c0, c1=c1: e.dma_start(out=wg[:, :, c0:c1], in_=wg_v[:, :, c0:c1]),
                 w=[("wg", g)], dma=f"wg{g}")
            P.op("pool", lambda e, c0=c0, c1=c1: e.dma_start(out=wu[:, :, c0:c1], in_=wu_v[:, :, c0:c1]),
                 w=[("wu", g)], dma=f"wu{g}")
        for g in range(2):
            P.op("pool", lambda e, g=g: e.dma_start(out=wd[:, g * 11:(g + 1) * 11, :], in_=wd_v[:, g * 11:(g + 1) * 11, :]),
                 w=[("wd", g)], dma=f"wd{g}")

        def wgrp(f):
            lo, hi = f * 128, f * 128 + 128
            return [g for g in range(len(GB) - 1) if GB[g] < hi and GB[g + 1] > lo]

        def load_h(n):
            s = n % 2
            P.op("sp", lambda e: e.dma_start(out=h2T[s][:], in_=H2T_d[:, :, n * 512:(n + 1) * 512].rearrange("c p t -> p c t")),
                 r=[("dram", "h2T", n)], w=[("h2Tb", s)], dma=f"h2Tb_{s}")

        xc = [0]

        def load_x2(n, t):
            k = xc[0] % 2
            xc[0] += 1
            tok = n * 512 + t * 128
            P.op("sp", lambda e: e.dma_start(out=x2t[k][:], in_=X2_d[tok:tok + 128, :]),
                 r=[("dram", "x2", tok)], w=[("x2t", k)], dma=f"x2t_{k}")
            return k

        gu = [0]
        dn = [0]

        def gate_up(n, f, a):
            s = n % 2
            i3 = gu[0] % 3
            gu[0] += 1
            bg = 2 * i3
            bu = 2 * i3 + 1
            pg = bank(bg)
            pu = bank(bu)
            for kc in range(8):
                P.op("pe", lambda e, kc=kc: e.matmul(pg, wg[:, kc, f * 128:(f + 1) * 128], h2T[s][:, kc, :],
                                                    start=(kc == 0), stop=(kc == 7)),
                     r=[("wg", g) for g in wgrp(f)] + [("h2Tb", s)], w=[("bank", bg)])
            for kc in range(8):
                P.op("pe", lambda e, kc=kc: e.matmul(pu, wu[:, kc, f * 128:(f + 1) * 128], h2T[s][:, kc, :],
                                                    start=(kc == 0), stop=(kc == 7)),
                     r=[("wu", g) for g in wgrp(f)] + [("h2Tb", s)], w=[("bank", bu)])
            j = gu[0] % 2
            P.op("act", lambda e: e.activation(out=sg[j][:], in_=pg, func=AF.Silu),
                 r=[("bank", bg)], w=[("sg", j)])
            P.op("dve", lambda e: e.tensor_tensor(out=actT[a][:, f, :], in0=pu, in1=sg[j][:], op=ALU.mult),
                 r=[("bank", bu), ("sg", j)], w=[("actT", a, f)])

        def down(n, a):
            ks_ = {0: load_x2(n, 0), 1: load_x2(n, 1)}
            for t in range(4):
                k = ks_[t]
                for hf in range(2):
                    bk = 6 + (dn[0] % 2)
                    dn[0] += 1
                    pb = bank(bk)
                    for f in range(NFC):
                        P.op("pe", lambda e, f=f, pb=pb, hf=hf, t=t: e.matmul(pb, actT[a][:, f, t * 128:(t + 1) * 128],
                                                                             wd[:, f, hf * 512:(hf + 1) * 512],
                                                                             start=(f == 0), stop=(f == NFC - 1)),
                             r=[("actT", a, f), ("wd", f // 11)], w=[("bank", bk)])
                    P.op("dve", lambda e, pb=pb, hf=hf, k=k: e.tensor_tensor(out=x2t[k][:, hf * 512:(hf + 1) * 512], in0=pb,
                                                                             in1=x2t[k][:, hf * 512:(hf + 1) * 512], op=ALU.add),
                         r=[("bank", bk)], w=[("x2t", k)])
                tok = n * 512 + t * 128
                P.op("sp", lambda e, k=k, tok=tok: e.dma_start(out=out_d[tok:tok + 128, :], in_=x2t[k][:]),
                     r=[("x2t", k)], w=[("dram", "out", tok)], dma=f"x2t_{k}")
                if t + 2 < 4:
                    ks_[t + 2] = load_x2(n, t + 2)

        load_h(0)
        load_h(1)
        for f in range(NFC):
            gate_up(0, f, 0)
            gate_up(1, f, 1)
        load_h(2)
        down(0, 0)
        down(1, 1)
        for n in range(2, NBLK):
            if n + 1 < NBLK:
                load_h(n + 1)
            for f in range(NFC):
                gate_up(n, f, n % 2)
            down(n, n % 2)

    if 1 in phases:
        phase1()
    P.barrier()
    A.reset(phase_mark)
    if 2 in phases:
        phase2()
    P.barrier()
    A.reset(phase_mark)
    A.top = top_save
    if 3 in phases:
        phase3()
    P.final_wait("sp")
    P.emit(nc)
    return nc


def _tab_index():
    idx = np.zeros((5, 128, 640), dtype=np.int64)
    PAD = 15 * 31
    p = np.arange(128)
    qi = np.arange(128)
    for v, r in enumerate([8, 0, 2, 60, 62]):
        ks = min(max(r - 4, 0), 56)
        nt = 5 if 4 <= r <= 58 else 4
        for t in range(5):
            kr = ks + 2 * t + p[:, None] // 64
            kc = p[:, None] % 64
            rq = r + qi[None, :] // 64
            qc = qi[None, :] % 64
            rs = np.clip(rq - 4, 0, 56)
            cs = np.clip(qc - 8, 0, 48)
            valid = (kr >= rs) & (kr < rs + 8) & (kc >= cs) & (kc < cs + 16) & (t < nt)
            ii = (kr - rq + 7) * 31 + (kc - qc + 15)
            idx[v, :, t * 128:(t + 1) * 128] = np.where(valid, ii, PAD)
    return idx


def _inv_tables():
    invF = np.zeros((4, 8), np.float32)
    invL = np.zeros((4, 8), np.float32)
    for g, w in enumerate((2, 4, 8, 16)):
        for j in range(8):
            t = j
            cnt = min(t + w // 2, SEQ) - max(t - w // 2, 0)
            invF[g, j] = 1.0 / cnt
            t = SEQ - 8 + j
            cnt = min(t + w // 2, SEQ) - max(t - w // 2, 0)
            invL[g, j] = 1.0 / cnt
    return (np.ascontiguousarray(np.broadcast_to(invF[None], (128, 4, 8))),
            np.ascontiguousarray(np.broadcast_to(invL[None], (128, 4, 8))))


def make_in_maps(x, norm1_g, w_in, q_norm_g, k_norm_g, rpb, w_pool, pool_scale, w_out,
                 norm2_g, w_gate, w_up, w_down):
    f = lambda a: np.ascontiguousarray(np.asarray(a, dtype=np.float32))
    x = f(x).reshape(N_CORES, TOK, D)
    rpb_pad = np.concatenate([f(rpb)[0].reshape(8, 15 * 31), np.full((8, 1), NEG, np.float32)], axis=1)
    idx = _tab_index()
    tab = np.ascontiguousarray(rpb_pad[:, idx].transpose(1, 2, 0, 3))
    invF, invL = _inv_tables()
    bdm = np.zeros((128, 128), np.float32)
    bdm[:64, :64] = 1.0
    bdm[64:, 64:] = 1.0
    common = dict(
        w_in=f(w_in)[0], w_out=f(w_out)[0], w_gate=f(w_gate)[0], w_up=f(w_up)[0], w_down=f(w_down)[0],
        w_pool=f(w_pool)[0],
        g1T=np.ascontiguousarray(f(norm1_g)[0].reshape(8, 128).T),
        g2T=np.ascontiguousarray(f(norm2_g)[0].reshape(8, 128).T),
        gq=np.ascontiguousarray(np.tile(f(q_norm_g)[0], 2).reshape(128, 1)),
        gk=np.ascontiguousarray(np.tile(f(k_norm_g)[0], 2).reshape(128, 1)),
        pscale=np.ascontiguousarray(f(pool_scale)[0].reshape(4, 128).T),
        tab=tab, ident=np.eye(128, dtype=np.float32), bd=bdm, invF=invF, invL=invL,
    )
    return [dict(common, x=np.ascontiguousarray(x[c])) for c in range(N_CORES)]


_NC_CACHE = {}


def kernel(x, norm1_g, w_in, q_norm_g, k_norm_g, rpb, w_pool, pool_scale, w_out,
           norm2_g, w_gate, w_up, w_down):
    in_maps = make_in_maps(x, norm1_g, w_in, q_norm_g, k_norm_g, rpb, w_pool, pool_scale, w_out,
                           norm2_g, w_gate, w_up, w_down)
    if "nc" not in _NC_CACHE:
        _NC_CACHE["nc"] = build_nc()
    nc = _NC_CACHE["nc"]
    res = run_bass_kernel_spmd(nc, in_maps, core_ids=list(range(N_CORES)))
    out = np.stack([np.asarray(r["out"], dtype=np.float32) for r in res.results], axis=0)
    return out.reshape(16, SEQ, D)
```
